# Optimizing a Trainium2 kernel written in Bass

```python
import jax, jax.numpy as jnp
from jax import lax
import numpy as np

D_MODEL = 1024
BATCH = 4
SEQ = 8192
DEPTH = 4
DEC_BATCH = 8
DEC_SEQ = 2048
PAST_LEN = 128

HEAD_DIM = 64
GRID_W = 64
A_HEADS = 8
A_KV_HEADS = 2
B_HEADS = 4
B_KV_HEADS = 2
C_HEADS = 4
Q_BLOCK = 128
WINDOW = 128
CHUNK = 128
ROPE_THETA = 10000.0
MEM_LEN = 256
X_HEADS = 4
X_HEAD_DIM = D_MODEL // X_HEADS
D_FF = ((8 * D_MODEL // 3 + 255) // 256) * 256
EPS = 1e-6
A_Q = A_HEADS * HEAD_DIM
A_KV = A_KV_HEADS * HEAD_DIM
B_Q = B_HEADS * HEAD_DIM
B_KV = B_KV_HEADS * HEAD_DIM
C_W = C_HEADS * HEAD_DIM
MIX_W = A_Q + B_Q + C_W
IN_SPLITS = (A_Q, A_KV, A_KV, B_Q, B_KV, B_KV, C_W, C_W, C_W, C_W)
IN_W = A_Q + 2 * A_KV + B_Q + 2 * B_KV + 4 * C_W
NEG_INF = -1e30

kernel_name = 'hybrid_bidir_parallel_heads_encoder'


def rmsnorm(x, g):
    xf = x.astype(jnp.float32)
    y = xf * lax.rsqrt(jnp.mean(xf * xf, axis=-1, keepdims=True) + EPS)
    return (y * g.astype(jnp.float32)).astype(x.dtype)


def rope_freqs(pos, dim):
    inv = ROPE_THETA ** (-jnp.arange(0, dim, 2, dtype=jnp.float32) / dim)
    return pos.astype(jnp.float32)[:, None] * inv[None, :]


def apply_rope(x, ang):
    half = x.shape[-1] // 2
    cos = jnp.cos(ang)[None, :, None, :]
    sin = jnp.sin(ang)[None, :, None, :]
    xf = x.astype(jnp.float32)
    x1, x2 = xf[..., :half], xf[..., half:]
    return jnp.concatenate([x1 * cos - x2 * sin, x1 * sin + x2 * cos], axis=-1).astype(x.dtype)


def axial_attention(q, k, v, g_q, g_k, ang):
    bsz, T, H, d = q.shape
    G = H // A_KV_HEADS
    q = apply_rope(rmsnorm(q, g_q), ang)
    k = apply_rope(rmsnorm(k, g_k), ang)
    scale = d ** -0.5
    qb = q.reshape(bsz, T // Q_BLOCK, Q_BLOCK, A_KV_HEADS, G, d).transpose(1, 0, 2, 3, 4, 5)

    def one_block(qi):
        s = jnp.einsum('bqkgd,bskd->bkgqs', qi, k, preferred_element_type=jnp.float32) * scale
        p = jax.nn.softmax(s, axis=-1).astype(v.dtype)
        return jnp.einsum('bkgqs,bskd->bqkgd', p, v)

    o = lax.map(one_block, qb)
    return o.transpose(1, 0, 2, 3, 4, 5).reshape(bsz, T, H * d)


def window_attention(q, k, v, sink, ang):
    bsz, T, H, d = q.shape
    K = B_KV_HEADS
    G = H // K
    W = WINDOW
    nb = T // W
    q = apply_rope(q, ang)
    k = apply_rope(k, ang)
    pad = ((0, 0), (W, W), (0, 0), (0, 0))
    kp = jnp.pad(k, pad).reshape(bsz, nb + 2, W, K, d)
    vp = jnp.pad(v, pad).reshape(bsz, nb + 2, W, K, d)
    kw = jnp.concatenate([kp[:, :-2], kp[:, 1:-1], kp[:, 2:]], axis=2)
    vw = jnp.concatenate([vp[:, :-2], vp[:, 1:-1], vp[:, 2:]], axis=2)
    qb = q.reshape(bsz, nb, W, K, G, d)
    s = jnp.einsum('bnqkgd,bnskd->bnkgqs', qb, kw, preferred_element_type=jnp.float32) * d ** -0.5
    blk = jnp.arange(nb)[:, None]
    qpos = blk * W + jnp.arange(W)[None, :]
    kpos = (blk - 1) * W + jnp.arange(3 * W)[None, :]
    rel = kpos[:, None, :] - qpos[:, :, None]
    valid = (jnp.abs(rel) <= W) & (kpos[:, None, :] >= 0) & (kpos[:, None, :] < T)
    s = jnp.where(valid[None, :, None, None], s, NEG_INF)
    sink_col = jnp.broadcast_to(sink.astype(jnp.float32).reshape(1, 1, K, G, 1, 1), s.shape[:-1] + (1,))
    p = jax.nn.softmax(jnp.concatenate([s, sink_col], axis=-1), axis=-1)[..., :-1].astype(v.dtype)
    o = jnp.einsum('bnkgqs,bnskd->bnqkgd', p, vw)
    return o.reshape(bsz, T, H * d)


def bidir_retention(q, k, v, gate, p_dec_f, p_dec_b, g_gn, ang):
    bsz, T, H, d = q.shape
    C = CHUNK
    nc = T // C
    f32 = jnp.float32
    qc = apply_rope(q, ang).astype(f32).reshape(bsz, nc, C, H, d)
    kc = (apply_rope(k, ang).astype(f32) * d ** -0.5).reshape(bsz, nc, C, H, d)
    vc = v.astype(f32).reshape(bsz, nc, C, H, d)
    lgf = -jnp.exp(p_dec_f.astype(f32))
    lgb = -jnp.exp(p_dec_b.astype(f32))
    idx = jnp.arange(C, dtype=f32)
    diff = idx[:, None] - idx[None, :]
    mask = jnp.where(diff >= 0,
                     jnp.exp(lgf[:, None, None] * jnp.maximum(diff, 0.0)),
                     jnp.exp(lgb[:, None, None] * jnp.maximum(-diff, 0.0)))
    s = jnp.einsum('bnahd,bnchd->bnhac', qc, kc) * mask
    o = jnp.einsum('bnhac,bnchd->bnahd', s, vc)
    zero = jnp.zeros((bsz, H, d, d), f32)
    kv_f = jnp.einsum('bnchd,hc,bnche->nbhde', kc, jnp.exp(lgf[:, None] * (C - 1 - idx)[None, :]), vc)
    dec_f = jnp.exp(lgf * C)[None, :, None, None]
    _, s_prev = lax.scan(lambda S, kv: (dec_f * S + kv, S), zero, kv_f)
    o = o + jnp.einsum('bnahd,ha,nbhde->bnahe', qc, jnp.exp(lgf[:, None] * (idx + 1.0)[None, :]), s_prev)
    kv_b = jnp.einsum('bnchd,hc,bnche->nbhde', kc, jnp.exp(lgb[:, None] * idx[None, :]), vc)
    dec_b = jnp.exp(lgb * C)[None, :, None, None]
    _, r_next = lax.scan(lambda R, kv: (dec_b * R + kv, R), zero, kv_b, reverse=True)
    o = o + jnp.einsum('bnahd,ha,nbhde->bnahe', qc, jnp.exp(lgb[:, None] * (C - idx)[None, :]), r_next)
    o = o.reshape(bsz, T, H, d)
    oc = o - jnp.mean(o, axis=-1, keepdims=True)
    o = oc * lax.rsqrt(jnp.mean(oc * oc, axis=-1, keepdims=True) + EPS)
    o = o.reshape(bsz, T, H * d) * g_gn.astype(f32)
    return (o * jax.nn.silu(gate.astype(f32))).astype(gate.dtype)


def memory_cross_attention(h, m, w_q, w_k, w_v, w_o):
    bsz, T, _ = h.shape
    q = (h @ w_q).reshape(bsz, T, X_HEADS, X_HEAD_DIM)
    k = (m @ w_k).reshape(bsz, -1, X_HEADS, X_HEAD_DIM)
    v = (m @ w_v).reshape(bsz, -1, X_HEADS, X_HEAD_DIM)
    s = jnp.einsum('bthd,bmhd->bhtm', q, k, preferred_element_type=jnp.float32) * X_HEAD_DIM ** -0.5
    p = jax.nn.softmax(s, axis=-1).astype(v.dtype)
    o = jnp.einsum('bhtm,bmhd->bthd', p, v).reshape(bsz, T, D_MODEL)
    return o @ w_o


def encoder_trunk(x, mem, g_mix, w_in, a_q_norm, a_k_norm, a_out_norm, b_sink, b_out_norm,
                  c_decay_fwd, c_decay_bwd, c_gn, w_out, g_cross, g_mem, w_xq, w_xk, w_xv, w_xo,
                  g_ffn, w_gate, w_up, w_down, g_final):
    bsz, T, _ = x.shape
    rows = T // GRID_W
    row = jnp.repeat(jnp.arange(rows), GRID_W)
    col = jnp.tile(jnp.arange(GRID_W), rows)
    ang_axial = jnp.concatenate([rope_freqs(row, HEAD_DIM // 2), rope_freqs(col, HEAD_DIM // 2)], axis=-1)
    ang_seq = rope_freqs(jnp.arange(T), HEAD_DIM)
    cuts = [int(c) for c in np.cumsum(IN_SPLITS)[:-1]]

    def heads(t):
        return t.reshape(bsz, T, -1, HEAD_DIM)

    for l in range(DEPTH):
        h = rmsnorm(x, g_mix[l])
        qa, ka, va, qb, kb, vb, qc, kc, vc, gc = jnp.split(h @ w_in[l], cuts, axis=-1)
        oa = rmsnorm(axial_attention(heads(qa), heads(ka), heads(va), a_q_norm[l], a_k_norm[l], ang_axial),
                     a_out_norm[l])
        ob = rmsnorm(window_attention(heads(qb), heads(kb), heads(vb), b_sink[l], ang_seq), b_out_norm[l])
        oc = bidir_retention(heads(qc), heads(kc), heads(vc), gc, c_decay_fwd[l], c_decay_bwd[l], c_gn[l], ang_seq)
        x = x + jnp.concatenate([oa, ob, oc], axis=-1) @ w_out[l]
        x = x + memory_cross_attention(rmsnorm(x, g_cross[l]), rmsnorm(mem, g_mem[l]),
                                       w_xq[l], w_xk[l], w_xv[l], w_xo[l])
        h = rmsnorm(x, g_ffn[l])
        x = x + (jax.nn.silu(h @ w_gate[l]) * (h @ w_up[l])) @ w_down[l]
    return rmsnorm(x, g_final)


def setup_inputs(seed: int = 0) -> dict:
    key = jax.random.key(seed)
    ks = iter(jax.random.split(key, 32))

    def nrm(shape, scale):
        return scale * jax.random.normal(next(ks), shape, jnp.float32)

    def gain(shape):
        return 1.0 + 0.05 * jax.random.normal(next(ks), shape, jnp.float32)

    base = jnp.asarray(np.log(-np.log(1.0 - 2.0 ** (-5.0 - np.arange(C_HEADS)))).astype(np.float32))
    L = DEPTH
    return {
        'x_prompt': nrm((BATCH, SEQ, D_MODEL), 1.0),
        'x_sample': nrm((DEC_BATCH, DEC_SEQ, D_MODEL), 1.0),
        'mem_prompt': nrm((BATCH, MEM_LEN, D_MODEL), 1.0),
        'mem_sample': nrm((DEC_BATCH, MEM_LEN, D_MODEL), 1.0),
        'g_mix': gain((L, D_MODEL)),
        'w_in': nrm((L, D_MODEL, IN_W), D_MODEL ** -0.5),
        'a_q_norm': gain((L, HEAD_DIM)),
        'a_k_norm': gain((L, HEAD_DIM)),
        'a_out_norm': gain((L, A_Q)),
        'b_sink': nrm((L, B_HEADS), 0.5),
        'b_out_norm': gain((L, B_Q)),
        'c_decay_fwd': base[None, :] + nrm((L, C_HEADS), 0.05),
        'c_decay_bwd': base[None, :] + nrm((L, C_HEADS), 0.05),
        'c_gn': gain((L, C_W)),
        'w_out': nrm((L, MIX_W, D_MODEL), MIX_W ** -0.5),
        'g_cross': gain((L, D_MODEL)),
        'g_mem': gain((L, D_MODEL)),
        'w_xq': nrm((L, D_MODEL, D_MODEL), D_MODEL ** -0.5),
        'w_xk': nrm((L, D_MODEL, D_MODEL), D_MODEL ** -0.5),
        'w_xv': nrm((L, D_MODEL, D_MODEL), D_MODEL ** -0.5),
        'w_xo': nrm((L, D_MODEL, D_MODEL), D_MODEL ** -0.5),
        'g_ffn': gain((L, D_MODEL)),
        'w_gate': nrm((L, D_MODEL, D_FF), D_MODEL ** -0.5),
        'w_up': nrm((L, D_MODEL, D_FF), D_MODEL ** -0.5),
        'w_down': nrm((L, D_FF, D_MODEL), D_FF ** -0.5),
        'g_final': gain((D_MODEL,)),
    }


def reference(x_prompt, x_sample, mem_prompt, mem_sample, g_mix, w_in, a_q_norm, a_k_norm, a_out_norm,
              b_sink, b_out_norm, c_decay_fwd, c_decay_bwd, c_gn, w_out, g_cross, g_mem, w_xq, w_xk,
              w_xv, w_xo, g_ffn, w_gate, w_up, w_down, g_final):
    y_prompt = encoder_trunk(x_prompt, mem_prompt, g_mix, w_in, a_q_norm, a_k_norm, a_out_norm, b_sink,
                             b_out_norm, c_decay_fwd, c_decay_bwd, c_gn, w_out, g_cross, g_mem, w_xq,
                             w_xk, w_xv, w_xo, g_ffn, w_gate, w_up, w_down, g_final)
    y_sample = encoder_trunk(x_sample, mem_sample, g_mix, w_in, a_q_norm, a_k_norm, a_out_norm, b_sink,
                             b_out_norm, c_decay_fwd, c_decay_bwd, c_gn, w_out, g_cross, g_mem, w_xq,
                             w_xk, w_xv, w_xo, g_ffn, w_gate, w_up, w_down, g_final)
    return (y_prompt, y_sample)
```

```python
import math
from contextlib import ExitStack

import numpy as np
import concourse.bass as bass
import concourse.mybir as mybir
from concourse.bass_utils import run_bass_kernel_spmd

F32 = mybir.dt.float32
BF16 = mybir.dt.bfloat16
I32 = mybir.dt.int32
AF = mybir.ActivationFunctionType
ALU = mybir.AluOpType
AX = mybir.AxisListType

D = 1024
KC = 8
DFF = 2816
FC = 22
MEM = 256
EPS = 1e-6
THETA = 10000.0
KRING = 8
ATTACH = True


class Buf:
    __slots__ = ("name", "w", "rs", "rd")

    def __init__(self, name):
        self.name = name
        self.w = None
        self.rs = {}
        self.rd = []


class Op:
    __slots__ = ("eng", "fn", "deps", "need", "val", "dma", "slot", "dval")


class Sched:
    ENG = ("pe", "act", "dve", "pool", "sp")

    def __init__(self, nc):
        self.nc = nc
        self.h = {"pe": nc.tensor, "act": nc.scalar, "dve": nc.vector, "pool": nc.gpsimd, "sp": nc.sync}
        self.ops = {e: [] for e in self.ENG}
        self.dmas = {e: [] for e in self.ENG}
        self.pending = []

    def op(self, eng, fn, r=(), w=(), dma=False):
        o = Op()
        o.eng, o.fn, o.dma, o.need, o.val = eng, fn, dma, False, 0
        deps = []
        for b in r:
            if b.w is not None:
                deps.append(b.w)
        for b in w:
            if b.w is not None and not b.rs and not b.rd:
                deps.append(b.w)
            deps.extend(b.rs.values())
            deps.extend(b.rd)
        if dma:
            n = len(self.dmas[eng])
            o.slot, o.dval = n % KRING, 16 * (n // KRING + 1)
            if n >= KRING:
                deps.append(self.dmas[eng][n - KRING])
            self.dmas[eng].append(o)
            self.pending.append(o)
        o.deps = [d for d in deps if d.dma or not (eng == "pe" and d.eng == "pe")]
        for d in o.deps:
            if not d.dma:
                d.need = True
        for b in r:
            if dma:
                b.rd.append(o)
            else:
                b.rs[eng] = o
        for b in w:
            b.w, b.rs, b.rd = o, {}, []
        self.ops[eng].append(o)
        return o

    def barrier(self):
        deps = list(self.pending)
        for e in self.ENG:
            for o in reversed(self.ops[e]):
                if o.fn is not None and not o.dma:
                    deps.append(o)
                    o.need = True
                    break
        self.pending = []
        for e in self.ENG:
            o = Op()
            o.eng, o.fn, o.dma, o.need, o.val, o.deps = e, None, False, False, 0, deps
            self.ops[e].append(o)

    def emit(self):
        nc = self.nc
        with ExitStack() as es:
            csem = {e: es.enter_context(nc.semaphore("c_" + e)) for e in self.ENG}
            dsem = {e: [es.enter_context(nc.semaphore("d_%s%d" % (e, i))) for i in range(KRING)]
                    for e in self.ENG if self.dmas[e]}
            for e in self.ENG:
                c = 0
                for o in self.ops[e]:
                    if o.need and not o.dma and o.fn is not None:
                        c += 1
                        o.val = c
            block = es.enter_context(nc.Block())
            names = {"pe": "tensor", "act": "scalar", "dve": "vector", "pool": "gpsimd", "sp": "sync"}

            def make(e):
                def body(h):
                    known = {}
                    for o in self.ops[e]:
                        waits = {}
                        for d in o.deps:
                            key, val = ((d.eng, d.slot), d.dval) if d.dma else ((d.eng, -1), d.val)
                            if known.get(key, 0) >= val:
                                continue
                            if waits.get(key, 0) < val:
                                waits[key] = val
                        wl = list(waits.items())
                        attach = None
                        if ATTACH and wl and o.fn is not None and not o.dma and e in ("act", "dve", "pool"):
                            attach = wl.pop()
                        for key, val in wl:
                            sem = dsem[key[0]][key[1]] if key[1] >= 0 else csem[key[0]]
                            h.wait_ge(sem, val)
                            known[key] = val
                        if o.fn is None:
                            continue
                        ins = o.fn(h)
                        if attach is not None:
                            key, val = attach
                            sem = dsem[key[0]][key[1]] if key[1] >= 0 else csem[key[0]]
                            ins._wait_ge(sem, val)
                            known[key] = val
                        if o.dma:
                            ins.then_inc(dsem[e][o.slot], 16)
                        elif o.need:
                            ins.then_inc(csem[e], 1)
                return body

            for e in self.ENG:
                getattr(block, names[e])(make(e))


class Seq:
    pass


def build(T1, T2, DEPTH):
    nc = bass.Bass("TRN2", target_bir_lowering=False)
    S = Sched(nc)
    es = ExitStack()

    def din(name, shape):
        return nc.dram_tensor(name, list(shape), F32, kind="ExternalInput").ap()

    def dscr(name, shape, dt):
        return nc.dram_tensor(name, list(shape), dt, kind="Internal").ap()

    L = DEPTH
    W = dict(
        g_mix=din("g_mix", (L, D)), w_in=din("w_in", (L, D, 2304)), a_q_norm=din("a_q_norm", (L, 64)),
        a_k_norm=din("a_k_norm", (L, 64)), a_out_norm=din("a_out_norm", (L, 512)), b_sink=din("b_sink", (L, 4)),
        b_out_norm=din("b_out_norm", (L, 256)), c_decay_fwd=din("c_decay_fwd", (L, 4)),
        c_decay_bwd=din("c_decay_bwd", (L, 4)), c_gn=din("c_gn", (L, 256)), w_out=din("w_out", (L, D, D)),
        g_cross=din("g_cross", (L, D)), g_mem=din("g_mem", (L, D)), w_xq=din("w_xq", (L, D, D)),
        w_xk=din("w_xk", (L, D, D)), w_xv=din("w_xv", (L, D, D)), w_xo=din("w_xo", (L, D, D)),
        g_ffn=din("g_ffn", (L, D)), w_gate=din("w_gate", (L, D, DFF)), w_up=din("w_up", (L, D, DFF)),
        w_down=din("w_down", (L, DFF, D)), g_final=din("g_final", (D,)),
    )
    seqs = []
    for si, T in enumerate((T1, T2)):
        q = Seq()
        q.T, q.nt, q.i = T, T // 128, si
        q.x_in = din("x%d" % si, (T, D))
        q.mem = din("m%d" % si, (MEM, D))
        q.y = nc.dram_tensor("y%d" % si, [T, D], F32, kind="ExternalOutput").ap()
        n = q.nt
        q.xs = dscr("xs%d" % si, (T, D), F32)
        q.qa = dscr("qa%d" % si, (n, 128, 512), BF16)
        q.ka = dscr("ka%d" % si, (n, 128, 128), BF16)
        q.va = dscr("va%d" % si, (T, 130), BF16)
        q.qb = dscr("qb%d" % si, (n, 128, 256), BF16)
        q.kb = dscr("kb%d" % si, (n, 128, 128), BF16)
        q.vb = dscr("vb%d" % si, (T, 130), BF16)
        q.qc = dscr("qc%d" % si, (n, 128, 256), BF16)
        q.kc = dscr("kc%d" % si, (n, 128, 256), BF16)
        q.kct = dscr("kct%d" % si, (T, 256), BF16)
        q.vcg = dscr("vcg%d" % si, (T, 512), BF16)
        q.mix = dscr("mix%d" % si, (T, D), BF16)
        seqs.append(q)
    NTM = max(q.nt for q in seqs)
    tab_s = dscr("tab_s", (NTM, 128, 256), F32)

    ARENA = 98000
    arena = es.enter_context(nc.sbuf_tensor("arena", [128, ARENA], BF16))
    psum = es.enter_context(nc.psum_tensor("psum", [128, 4096], F32))
    PB = [Buf("psb%d" % i) for i in range(8)]
    st = {"off": 0, "n": 0}

    def sb(shape, dt, name=None):
        n = int(np.prod(shape[1:]))
        nb = n * (2 if dt in (F32, I32) else 1)
        nb_al = (nb + 15) // 16 * 16
        off = st["off"]
        assert off + nb_al <= ARENA, "SBUF arena overflow %d" % (off + nb_al)
        st["off"] = off + nb_al
        v = arena[0:shape[0], off:off + nb]
        if dt != BF16:
            v = v.bitcast(dt)
        if len(shape) == 3:
            v = v.rearrange("p (a b) -> p a b", b=shape[2])
        elif len(shape) == 4:
            v = v.rearrange("p (a b c) -> p a b c", b=shape[2], c=shape[3])
        st["n"] += 1
        return v, Buf(name or "t%d" % st["n"])

    def ps(bank, nb=1, dt=F32):
        v = psum[:, bank * 512:(bank + nb) * 512]
        return v.bitcast(dt) if dt != F32 else v

    def mm(out, lhsT, rhs, start, stop, r, w):
        return S.op("pe", lambda e: e.matmul(out, lhsT=lhsT, rhs=rhs, start=start, stop=stop), r, w)

    def tr(out, in_, ident, r, w):
        return S.op("pe", lambda e: e.transpose(out=out, in_=in_, identity=ident), r, w)

    def act(out, in_, func, r, w, **kw):
        return S.op("act", lambda e: e.activation(out=out, in_=in_, func=func, **kw), r, w)

    def tt(eng, out, in0, in1, op, r, w):
        return S.op(eng, lambda e: e.tensor_tensor(out=out, in0=in0, in1=in1, op=op), r, w)

    def ts(eng, out, in0, s1, s2, op0, op1, r, w):
        if op1 is None:
            return S.op(eng, lambda e: e.tensor_scalar(out=out, in0=in0, scalar1=s1, scalar2=None, op0=op0), r, w)
        return S.op(eng, lambda e: e.tensor_scalar(out=out, in0=in0, scalar1=s1, scalar2=s2, op0=op0, op1=op1), r, w)

    def stt(eng, out, in0, scalar, in1, op0, op1, r, w):
        return S.op(eng, lambda e: e.scalar_tensor_tensor(out=out, in0=in0, scalar=scalar, in1=in1, op0=op0, op1=op1), r, w)

    def cp(eng, out, in_, r, w):
        if eng == "act":
            return act(out, in_, AF.Copy, r, w)
        return S.op(eng, lambda e: e.tensor_copy(out=out, in_=in_), r, w)

    def recip(out, in_, r, w):
        return S.op("dve", lambda e: e.reciprocal(out=out, in_=in_), r, w)

    def rsum(out, in_, r, w):
        return S.op("dve", lambda e: e.reduce_sum(out=out, in_=in_, axis=AX.X), r, w)

    def memset(eng, out, val, w):
        return S.op(eng, lambda e: e.memset(out, val), (), w)

    def iota(out, pattern, base, cm, w):
        return S.op("pool", lambda e: e.iota(out, pattern=pattern, base=base, channel_multiplier=cm,
                                             allow_small_or_imprecise_dtypes=True), (), w)

    def dma(q, out, in_, r, w):
        return S.op(q, lambda e: e.dma_start(out=out, in_=in_), r, w, dma=True)

    def bc(ap, shape):
        return ap.to_broadcast(list(shape))

    ident_b, Bidb = sb([128, 128], BF16, "ident_b")
    ident_f, Bidf = sb([128, 128], F32, "ident_f")
    ones_b, Bones = sb([128, 128], BF16, "ones_b")
    epsb, Beps = sb([128, 1], F32, "eps")
    triL, BtriL = sb([128, 128], BF16, "triL")
    triR, BtriR = sb([128, 128], BF16, "triR")
    posD, BposD = sb([128, 128], F32, "posD")
    negD, BnegD = sb([128, 128], F32, "negD")
    aidx1, Ba1 = sb([128, 128], F32, "aidx1")
    aidxC, BaC = sb([128, 128], F32, "aidxC")
    cidx, Bcidx = sb([128, 1], F32, "cidx")
    c127, Bc127 = sb([128, 1], F32, "c127")
    PERSIST = st["off"]

    def setup():
        Dm, BD = sb([128, 128], F32, "Dm")
        iota(Dm, [[1, 128]], 0, -1, [BD])
        ts("dve", ident_f, Dm, 0.0, None, ALU.is_equal, None, [BD], [Bidf])
        ts("dve", ident_b, Dm, 0.0, None, ALU.is_equal, None, [BD], [Bidb])
        ts("dve", triL, Dm, 0.0, None, ALU.is_le, None, [BD], [BtriL])
        ts("dve", triR, Dm, 0.0, None, ALU.is_ge, None, [BD], [BtriR])
        ts("dve", posD, Dm, 0.0, None, ALU.max, None, [BD], [BposD])
        ts("dve", negD, Dm, -1.0, 0.0, ALU.mult, ALU.max, [BD], [BnegD])
        memset("pool", ones_b, 1.0, [Bones])
        memset("pool", epsb, EPS, [Beps])
        iota(aidx1, [[1, 128]], 1, 0, [Ba1])
        iota(aidxC, [[-1, 128]], 128, 0, [BaC])
        iota(cidx, [[1, 1]], 0, 1, [Bcidx])
        iota(c127, [[1, 1]], 127, -1, [Bc127])
        NT = NTM
        pos, Bpos = sb([128, NT], F32)
        iota(pos, [[128, NT]], 0, 1, [Bpos])
        hi, Bhi = sb([128, 1], F32)
        ts("dve", hi, cidx, 64.0, None, ALU.is_ge, None, [Bcidx], [Bhi])
        colv, Bcol = sb([128, 1], F32)
        stt("dve", colv, hi, -64.0, cidx, ALU.mult, ALU.add, [Bhi, Bcidx], [Bcol])
        rowv, Brow = sb([128, NT], F32)
        iota(rowv, [[2, NT]], 0, 0, [Brow])
        ts("dve", rowv, rowv, hi[:, 0:1], None, ALU.add, None, [Brow, Bhi], [Brow])
        i32t, Bi32 = sb([128, 32], F32)
        iota(i32t, [[1, 32]], 0, 0, [Bi32])
        inv32, Binv32 = sb([128, 32], F32)
        act(inv32, i32t, AF.Exp, [Bi32], [Binv32], scale=-math.log(THETA) / 32.0)
        inv16, Binv16 = sb([128, 16], F32)
        act(inv16, i32t[:, 0:16], AF.Exp, [Bi32], [Binv16], scale=-math.log(THETA) / 16.0)
        angs, Bangs = sb([128, NT, 32], F32)
        tt("dve", angs, bc(pos.unsqueeze(2), [128, NT, 32]), bc(inv32.unsqueeze(1), [128, NT, 32]), ALU.mult,
           [Bpos, Binv32], [Bangs])
        anga, Banga = sb([128, NT, 32], F32)
        tt("dve", anga[:, :, 0:16], bc(rowv.unsqueeze(2), [128, NT, 16]), bc(inv16.unsqueeze(1), [128, NT, 16]),
           ALU.mult, [Brow, Binv16], [Banga])
        ts("dve", anga[:, :, 16:32], bc(inv16.unsqueeze(1), [128, NT, 16]), colv[:, 0:1], None, ALU.mult, None,
           [Binv16, Bcol, Banga], [Banga])
        tab, Btab = sb([128, NT, 256], F32)
        u, Bu = sb([128, NT, 32], F32)
        ni, Bni = sb([128, NT, 32], I32)
        nf, Bnf = sb([128, NT, 32], F32)
        sn, Bsn = sb([128, NT, 32], F32)

        def sinshift(ang, Bang, shift):
            ts("dve", u, ang, 1.0 / (2 * math.pi), 0.5 + shift / (2 * math.pi), ALU.mult, ALU.add, [Bang], [Bu])
            cp("dve", ni, u, [Bu], [Bni])
            cp("dve", nf, ni, [Bni], [Bnf])
            tt("dve", u, u, nf, ALU.subtract, [Bu, Bnf], [Bu])
            ts("dve", nf, u, 0.0, None, ALU.is_lt, None, [Bu], [Bnf])
            tt("dve", u, u, nf, ALU.add, [Bu, Bnf], [Bu])
            ts("dve", nf, u, 1.0, None, ALU.is_ge, None, [Bu], [Bnf])
            tt("dve", u, u, nf, ALU.subtract, [Bu, Bnf], [Bu])
            ts("dve", u, u, 2 * math.pi, -math.pi, ALU.mult, ALU.add, [Bu], [Bu])
            ts("dve", u, u, -3.14159, 3.14159, ALU.max, ALU.min, [Bu], [Bu])
            act(sn, u, AF.Sin, [Bu], [Bsn])

        for base, ang, Bang in ((0, anga, Banga), (128, angs, Bangs)):
            sinshift(ang, Bang, math.pi / 2)
            cp("dve", tab[:, :, base:base + 32], sn, [Bsn], [Btab])
            cp("dve", tab[:, :, base + 32:base + 64], sn, [Bsn], [Btab])
            sinshift(ang, Bang, 0.0)
            ts("dve", tab[:, :, base + 64:base + 96], sn, -1.0, None, ALU.mult, None, [Bsn], [Btab])
            cp("dve", tab[:, :, base + 96:base + 128], sn, [Bsn], [Btab])
        dma("sp", tab_s.rearrange("t p c -> p t c"), tab, [Btab], [])
        S.barrier()

    def rmsnorm_to_bf16(xt, Bx, gain, Bg, hb, Bhb, scr):
        junk, Bj, ss, Bss, rstd, Brs = scr
        act(junk, xt, AF.Square, [Bx], [Bj, Bss], accum_out=ss)
        act(ss, ss, AF.Sqrt, [Bss, Beps], [Bss], scale=1.0 / D, bias=epsb)
        recip(rstd, ss, [Bss], [Brs])
        stt("dve", hb, xt, rstd[:, 0:1], gain, ALU.mult, ALU.mult, [Bx, Brs, Bg], [Bhb])

    def transpose8(src, Bsrc, dst, Bdst, bank, cpeng):
        pt = ps(bank, 1, BF16)
        for k in range(KC):
            tr(pt[:, k * 128:(k + 1) * 128], src[:, k * 128:(k + 1) * 128], ident_b, [Bsrc, Bidb], [PB[bank]])
        cp(cpeng, dst, pt.rearrange("p (k t) -> p k t", t=128), [PB[bank]], [Bdst])

    def load_w(dst, Bdst, src2d, nk):
        dma("pool", dst, src2d.rearrange("(k p) n -> p k n", p=128), [], [Bdst])

    def load_bc(dst, Bdst, src1d):
        dma("sp", dst, src1d.partition_broadcast(128), [], [Bdst])

    def phaseA(l):
        st["off"] = PERSIST
        win, Bwin = sb([128, KC, 2304], BF16, "win")
        wi = W["w_in"][l]
        for k in range(KC):
            rows = wi[k * 128:(k + 1) * 128, :]
            for j in range(4):
                dma("pool", win[:, k, j * 128:(j + 1) * 128].rearrange("p (g d) -> p g d", g=2),
                    rows[:, 0:512].rearrange("p (g j d) -> p j g d", g=2, j=4)[:, j, :, :], [], [Bwin])
            for j in range(2):
                dma("pool", win[:, k, 640 + j * 128:640 + (j + 1) * 128].rearrange("p (g d) -> p g d", g=2),
                    rows[:, 768:1024].rearrange("p (g j d) -> p j g d", g=2, j=2)[:, j, :, :], [], [Bwin])
        for (d0, d1, s0) in ((512, 640, 512), (896, 1024, 1024), (1024, 1536, 1280), (1536, 1664, 640),
                             (1664, 1792, 1152), (1792, 2304, 1792)):
            dma("pool", win[:, :, d0:d1], wi[:, s0:s0 + (d1 - d0)].rearrange("(k p) n -> p k n", p=128), [], [Bwin])
        gmix, Bgmix = sb([128, D], F32)
        load_bc(gmix, Bgmix, W["g_mix"][l])
        gqk, Bgqk = sb([128, 10, 64], F32)
        dma("sp", gqk[:, 0:8, :], bc(W["a_q_norm"][l].partition_broadcast(128).unsqueeze(1), [128, 8, 64]), [], [Bgqk])
        dma("sp", gqk[:, 8:10, :], bc(W["a_k_norm"][l].partition_broadcast(128).unsqueeze(1), [128, 2, 64]), [], [Bgqk])
        xt = [sb([128, D], F32) for _ in range(2)]
        tabt = [sb([128, 256], F32) for _ in range(2)]
        junk = sb([128, D], BF16)
        ss = sb([128, 1], F32)
        rstd = sb([128, 1], F32)
        scr = (junk[0], junk[1], ss[0], ss[1], rstd[0], rstd[1])
        hb, Bhb = sb([128, D], BF16)
        hT, BhT = sb([128, KC, 128], BF16)
        pf, Bpf = sb([128, 2304], F32)
        sq, Bsq = sb([128, 640], F32)
        ssh, Bssh = sb([128, 10], F32)
        rsh, Brsh = sb([128, 10], F32)
        ra, Bra = sb([128, 896], F32)
        rb, Brb = sb([128, 896], F32)
        pb = [sb([128, 1536], BF16) for _ in range(2)]
        vaa = [sb([128, 2, 65], BF16) for _ in range(2)]
        vab = [sb([128, 2, 65], BF16) for _ in range(2)]
        vcg = [sb([128, 512], BF16) for _ in range(2)]
        pT = [sb([128, 12, 128], BF16) for _ in range(2)]
        for v_, B_ in vaa + vab:
            memset("pool", v_, 1.0, [B_])
        cnt = 0
        for q in seqs:
            src = q.x_in if l == 0 else q.xs
            for t in range(q.nt):
                s = cnt % 2
                cnt += 1
                x_, Bx = xt[s]
                tb, Btb = tabt[s]
                dma("sp", x_, src[t * 128:(t + 1) * 128, :], [], [Bx])
                dma("sp", tb, tab_s[t], [], [Btb])
                rmsnorm_to_bf16(x_, Bx, gmix, Bgmix, hb, Bhb, scr)
                transpose8(hb, Bhb, hT, BhT, 0, "act")
                for cg in range(5):
                    c0 = cg * 512
                    cw = 512 if cg < 4 else 256
                    for k in range(KC):
                        mm(ps(1 + cg)[:, 0:cw], hT[:, k, :], win[:, k, c0:c0 + cw], k == 0, k == KC - 1,
                           [BhT, Bwin], [PB[1 + cg]])
                    cp("act", pf[:, c0:c0 + cw], ps(1 + cg)[:, 0:cw], [PB[1 + cg]], [Bpf])
                act(sq, pf[:, 0:640], AF.Square, [Bpf], [Bsq])
                rsum(ssh, sq.rearrange("p (h d) -> p h d", d=64), [Bsq], [Bssh])
                act(ssh, ssh, AF.Sqrt, [Bssh, Beps], [Bssh], scale=1.0 / 64, bias=epsb)
                recip(rsh, ssh, [Bssh], [Brsh])
                pf3 = pf[:, 0:640].rearrange("p (h d) -> p h d", d=64)
                tt("dve", pf3, pf3, bc(rsh.unsqueeze(2), [128, 10, 64]), ALU.mult, [Bpf, Brsh], [Bpf])
                tt("dve", pf3, pf3, gqk, ALU.mult, [Bpf, Bgqk], [Bpf])
                p_, Bp = pb[s]
                for (c0, nh, tb0) in ((0, 10, 0), (640, 14, 128)):
                    xv = pf[:, c0:c0 + nh * 64].rearrange("p (h d) -> p h d", d=64)
                    av = ra[:, 0:nh * 64].rearrange("p (h d) -> p h d", d=64)
                    bv = rb[:, 0:nh * 64].rearrange("p (h d) -> p h d", d=64)
                    CC = bc(tb[:, tb0:tb0 + 64].unsqueeze(1), [128, nh, 64])
                    nS = bc(tb[:, tb0 + 64:tb0 + 96].unsqueeze(1), [128, nh, 32])
                    pS = bc(tb[:, tb0 + 96:tb0 + 128].unsqueeze(1), [128, nh, 32])
                    tt("dve", av, xv, CC, ALU.mult, [Bpf, Btb], [Bra])
                    tt("pool", bv[:, :, 0:32], xv[:, :, 32:64], nS, ALU.mult, [Bpf, Btb], [Brb])
                    tt("pool", bv[:, :, 32:64], xv[:, :, 0:32], pS, ALU.mult, [Bpf, Btb, Brb], [Brb])
                    tt("dve", p_[:, c0:c0 + nh * 64].rearrange("p (h d) -> p h d", d=64), av, bv, ALU.add,
                       [Bra, Brb], [Bp])
                va_, Bva = vaa[s]
                vb_, Bvb = vab[s]
                vc_, Bvc = vcg[s]
                cp("act", va_[:, :, 0:64], pf[:, 1536:1664].rearrange("p (g d) -> p g d", d=64), [Bpf], [Bva])
                cp("act", vb_[:, :, 0:64], pf[:, 1664:1792].rearrange("p (g d) -> p g d", d=64), [Bpf], [Bvb])
                cp("act", vc_[:, 0:256], pf[:, 1792:2048], [Bpf], [Bvc])
                act(vc_[:, 256:512], pf[:, 2048:2304], AF.Silu, [Bpf], [Bvc])
                pt2 = ps(6, 2, BF16)
                T_, BT = pT[s]
                for j in range(12):
                    bank = 6 if j < 8 else 7
                    tr(pt2[:, j * 128:(j + 1) * 128], p_[:, j * 128:(j + 1) * 128], ident_b, [Bp, Bidb], [PB[bank]])
                cp("dve", T_[:, 0:8, :], pt2[:, 0:1024].rearrange("p (k t) -> p k t", t=128), [PB[6]], [BT])
                cp("dve", T_[:, 8:12, :], pt2[:, 1024:1536].rearrange("p (k t) -> p k t", t=128), [PB[7]], [BT])
                dma("pool", q.qa[t].rearrange("p (j t) -> p j t", t=128), T_[:, 0:4, :], [BT], [])
                dma("pool", q.ka[t], T_[:, 4, :], [BT], [])
                dma("pool", q.qb[t].rearrange("p (j t) -> p j t", t=128), T_[:, 5:7, :], [BT], [])
                dma("pool", q.kb[t], T_[:, 7, :], [BT], [])
                dma("pool", q.qc[t].rearrange("p (j t) -> p j t", t=128), T_[:, 8:10, :], [BT], [])
                dma("pool", q.kc[t].rearrange("p (j t) -> p j t", t=128), T_[:, 10:12, :], [BT], [])
                rows = slice(t * 128, (t + 1) * 128)
                dma("pool", q.kct[rows, :], p_[:, 1280:1536], [Bp], [])
                dma("pool", q.va[rows, :], va_.rearrange("p g d -> p (g d)"), [Bva], [])
                dma("pool", q.vb[rows, :], vb_.rearrange("p g d -> p (g d)"), [Bvb], [])
                dma("pool", q.vcg[rows, :], vc_, [Bvc], [])
        S.barrier()

    def attn(l, q, kind):
        st["off"] = PERSIST
        nt = q.nt
        nj = 4 if kind == "a" else 2
        qw = nj * 128
        H = 2 * nj
        qs, ks, vs = (q.qa, q.ka, q.va) if kind == "a" else (q.qb, q.kb, q.vb)
        col0 = 0 if kind == "a" else 512
        KT0, BKT0 = sb([128, nt, 128], BF16)
        KT1, BKT1 = sb([128, nt, 128], BF16)
        VA, BVA = sb([128, nt, 130], BF16)
        memset("pool", KT0[64:128], 0.0, [BKT0])
        memset("pool", KT1[0:64], 0.0, [BKT1])
        ksr = ks.rearrange("t p c -> p t c")
        dma("sp", KT0[0:64], ksr[0:64], [], [BKT0])
        dma("sp", KT1[64:128], ksr[64:128], [], [BKT1])
        dma("sp", VA, vs.rearrange("(t p) c -> p t c", p=128), [], [BVA])
        gout, Bgout = sb([128, H * 64], F32)
        load_bc(gout, Bgout, (W["a_out_norm"] if kind == "a" else W["b_out_norm"])[l])
        if kind == "b":
            esk, Besk = sb([128, 4], F32)
            load_bc(esk, Besk, W["b_sink"][l])
            act(esk, esk, AF.Exp, [Besk], [Besk])
        QT = [sb([128, qw], BF16) for _ in range(2)]
        PT = [sb([128, 2 * qw], BF16) for _ in range(3)]
        oT, BoT = sb([65, 2, qw], F32)
        o_, Bo = sb([128, H, 64], F32)
        den, Bden = sb([128, H], F32)
        junk, Bj = sb([128, H * 64], BF16)
        ss, Bss = sb([128, 1], F32)
        rstd, Brs = sb([128, 1], F32)
        mixt = [sb([128, H * 64], BF16) for _ in range(2)]
        nsb = 2 if kind == "a" else 1
        steps = []
        for i in range(nt):
            klist = list(range(nt)) if kind == "a" else [c for c in (i - 1, i, i + 1) if 0 <= c < nt]
            for idx, c in enumerate(klist):
                steps.append((i, idx, c, idx == len(klist) - 1))

        def emit_S(n):
            i, idx, c, lastc = steps[n]
            Q_, BQ = QT[i % 2]
            if idx == 0:
                dma("sp", Q_, qs[i], [], [BQ])
            b0 = (n % 2) * nsb
            pss = ps(b0, nsb)
            banks = [PB[b0 + x] for x in range(nsb)]
            mm(pss[:, 0:qw], KT0[:, c, :], Q_, True, True, [BKT0, BQ], banks)
            mm(pss[:, qw:2 * qw], KT1[:, c, :], Q_, True, True, [BKT1, BQ], banks)

        def emit_rest(n):
            i, idx, c, lastc = steps[n]
            b0 = (n % 2) * nsb
            pss = ps(b0, nsb)
            banks = [PB[b0 + x] for x in range(nsb)]
            P_, BP = PT[n % 3]
            act(P_, pss[:, 0:2 * qw], AF.Exp, banks, [BP], scale=0.125)
            if kind == "b" and c != i:
                tri = triL if c < i else triR
                Btri = BtriL if c < i else BtriR
                P3 = P_.rearrange("p (h t) -> p h t", t=128)
                tt("dve", P3, P3, bc(tri.unsqueeze(1), [128, 2 * nj, 128]), ALU.mult, [BP, Btri], [BP])
            for g in range(2):
                mm(ps(4 + g)[0:65, 0:qw], VA[:, c, g * 65:(g + 1) * 65], P_[:, g * qw:(g + 1) * qw],
                   idx == 0, lastc, [BVA, BP], [PB[4 + g]])
            while pend and (lastc or n - pend[0][1] >= 2):
                epilogue2(pend.pop(0)[0])
            if lastc:
                epilogue(i)
                pend.append((i, n))

        pend = []

        def epilogue(i):
            for g in range(2):
                cp("dve", oT[:, g, :], ps(4 + g)[0:65, 0:qw], [PB[4 + g]], [BoT])
            for g in range(2):
                for j in range(nj):
                    tr(ps(6 + g)[:, j * 65:(j + 1) * 65], oT[0:65, g, j * 128:(j + 1) * 128], ident_f[0:65, 0:65],
                       [BoT, Bidf], [PB[6 + g]])
            for g in range(2):
                pe3 = ps(6 + g)[:, 0:nj * 65].rearrange("p (j c) -> p j c", c=65)
                dg = den[:, g * nj:(g + 1) * nj]
                if kind == "b":
                    tt("dve", dg, pe3[:, :, 64], esk[:, g * nj:(g + 1) * nj], ALU.add, [PB[6 + g], Besk], [Bden])
                    recip(dg, dg, [Bden], [Bden])
                else:
                    recip(dg, pe3[:, :, 64], [PB[6 + g]], [Bden])
                tt("dve", o_[:, g * nj:(g + 1) * nj, :], pe3[:, :, 0:64], bc(dg.unsqueeze(2), [128, nj, 64]), ALU.mult,
                   [PB[6 + g], Bden], [Bo])

        def epilogue2(i):
            of = o_.rearrange("p h d -> p (h d)")
            act(junk, of, AF.Square, [Bo], [Bj, Bss], accum_out=ss)
            act(ss, ss, AF.Sqrt, [Bss, Beps], [Bss], scale=1.0 / (H * 64), bias=epsb)
            recip(rstd, ss, [Bss], [Brs])
            m_, Bm = mixt[i % 2]
            stt("dve", m_, of, rstd[:, 0:1], gout, ALU.mult, ALU.mult, [Bo, Brs, Bgout], [Bm])
            dma("pool", q.mix[i * 128:(i + 1) * 128, col0:col0 + H * 64], m_, [Bm], [])

        emit_S(0)
        for n in range(len(steps)):
            if n + 1 < len(steps):
                emit_S(n + 1)
            emit_rest(n)
        while pend:
            epilogue2(pend.pop(0)[0])
        S.barrier()

    def retention(l, q):
        st["off"] = PERSIST
        nt = q.nt
        lgf, Blgf = sb([128, 4], F32)
        lgb, Blgb = sb([128, 4], F32)
        load_bc(lgf, Blgf, W["c_decay_fwd"][l])
        load_bc(lgb, Blgb, W["c_decay_bwd"][l])
        for t_, B_ in ((lgf, Blgf), (lgb, Blgb)):
            act(t_, t_, AF.Exp, [B_], [B_])
            ts("dve", t_, t_, -1.0, None, ALU.mult, None, [B_], [B_])
        lpf, Blpf = sb([128, 2], F32)
        lpb, Blpb = sb([128, 2], F32)
        for (dst, Bd, srcv, Bs) in ((lpf, Blpf, lgf, Blgf), (lpb, Blpb, lgb, Blgb)):
            s3 = srcv.rearrange("p (a b) -> p a b", b=2)
            cp("dve", dst[0:64, :], s3[0:64, :, 0], [Bs], [Bd])
            cp("dve", dst[64:128, :], s3[64:128, :, 1], [Bs, Bd], [Bd])
        t1, Bt1 = sb([128, 128], F32)
        t2, Bt2 = sb([128, 128], F32)
        maskT, Bmask = sb([128, 4, 128], BF16)
        for h in range(4):
            ts("dve", t1, posD, lgf[:, h:h + 1], None, ALU.mult, None, [BposD, Blgf], [Bt1])
            stt("dve", t2, negD, lgb[:, h:h + 1], t1, ALU.mult, ALU.add, [BnegD, Blgb, Bt1], [Bt2])
            act(t2, t2, AF.Exp, [Bt2], [Bt2])
            ts("dve", maskT[:, h, :], t2, 0.125, None, ALU.mult, None, [Bt2], [Bmask])
        qdf, Bqdf = sb([128, 2, 128], BF16)
        qdb, Bqdb = sb([128, 2, 128], BF16)
        for pr in range(2):
            act(t1, aidx1, AF.Exp, [Ba1, Blpf], [Bt1], scale=lpf[:, pr:pr + 1])
            ts("dve", qdf[:, pr, :], t1, 0.125, None, ALU.mult, None, [Bt1], [Bqdf])
            act(t2, aidxC, AF.Exp, [BaC, Blpb], [Bt2], scale=lpb[:, pr:pr + 1])
            ts("dve", qdb[:, pr, :], t2, 0.125, None, ALU.mult, None, [Bt2], [Bqdb])
        kdf, Bkdf = sb([128, 4], F32)
        kdb, Bkdb = sb([128, 4], F32)
        ts("dve", kdf, lgf, c127[:, 0:1], None, ALU.mult, None, [Blgf, Bc127], [Bkdf])
        act(kdf, kdf, AF.Exp, [Bkdf], [Bkdf])
        ts("dve", kdb, lgb, cidx[:, 0:1], None, ALU.mult, None, [Blgb, Bcidx], [Bkdb])
        act(kdb, kdb, AF.Exp, [Bkdb], [Bkdb])
        dCf, BdCf = sb([128, 2], F32)
        dCb, BdCb = sb([128, 2], F32)
        act(dCf, lpf, AF.Exp, [Blpf], [BdCf], scale=128.0)
        act(dCb, lpb, AF.Exp, [Blpb], [BdCb], scale=128.0)
        cgn, Bcgn = sb([128, 256], F32)
        load_bc(cgn, Bcgn, W["c_gn"][l])
        Sall, BSall = sb([128, nt, 128], BF16)
        kvbA, BkvbA = sb([128, nt, 128], F32)
        Sst, BS = sb([128, 128], F32)
        Rst, BR = sb([128, 128], F32)
        Rb, BRb = sb([128, 128], BF16)
        memset("pool", Sst, 0.0, [BS])
        memset("pool", Rst, 0.0, [BR])
        kt = [sb([128, 256], BF16) for _ in range(2)]
        vt = [sb([128, 512], BF16) for _ in range(2)]
        kwf, Bkwf = sb([128, 256], BF16)
        kwb, Bkwb = sb([128, 256], BF16)
        for n in range(nt):
            k_, Bk = kt[n % 2]
            v_, Bv = vt[n % 2]
            rows = slice(n * 128, (n + 1) * 128)
            dma("sp", k_, q.kct[rows, :], [], [Bk])
            dma("sp", v_, q.vcg[rows, :], [], [Bv])
            k3 = k_.rearrange("p (h d) -> p h d", d=64)
            tt("dve", kwf.rearrange("p (h d) -> p h d", d=64), k3, bc(kdf.unsqueeze(2), [128, 4, 64]), ALU.mult,
               [Bk, Bkdf], [Bkwf])
            tt("pool", kwb.rearrange("p (h d) -> p h d", d=64), k3, bc(kdb.unsqueeze(2), [128, 4, 64]), ALU.mult,
               [Bk, Bkdb], [Bkwb])
            for pr in range(2):
                mm(ps(0)[:, pr * 256:(pr + 1) * 256], kwf[:, pr * 128:(pr + 1) * 128], v_[:, 0:256], True, True,
                   [Bkwf, Bv], [PB[0]])
                mm(ps(1)[:, pr * 256:(pr + 1) * 256], kwb[:, pr * 128:(pr + 1) * 128], v_[:, 0:256], True, True,
                   [Bkwb, Bv], [PB[1]])
            cp("act", Sall[:, n, :], Sst, [BS], [BSall])
            S3 = Sst.rearrange("p (a e) -> p a e", e=64)
            tt("dve", S3, S3, bc(dCf.unsqueeze(2), [128, 2, 64]), ALU.mult, [BS, BdCf, BSall], [BS])
            for h2 in range(2):
                r_ = slice(h2 * 64, (h2 + 1) * 64)
                blkf = psum[r_, h2 * 64:h2 * 64 + 768].rearrange("p (a x) -> p a x", x=384)[:, :, 0:64]
                blkb = psum[r_, 512 + h2 * 64:512 + h2 * 64 + 768].rearrange("p (a x) -> p a x", x=384)[:, :, 0:64]
                tt("dve", S3[r_], S3[r_], blkf, ALU.add, [BS, PB[0]], [BS])
                cp("act", kvbA[r_, n, :].rearrange("p (a e) -> p a e", e=64), blkb, [PB[1]], [BkvbA])
        qz = [[sb([128, 256], BF16) for _ in range(2)] for _ in range(2)]
        kT = [sb([128, 256], BF16) for _ in range(2)]
        vg = [sb([128, 512], BF16) for _ in range(2)]
        Qfz = [sb([128, 256], BF16) for _ in range(2)]
        Qbz = [sb([128, 256], BF16) for _ in range(2)]
        for h2 in range(2):
            oth = slice((1 - h2) * 64, (2 - h2) * 64)
            for (t_, B_) in qz[h2] + [Qfz[h2], Qbz[h2]]:
                memset("pool", t_[oth], 0.0, [B_])
        PTm, BPTm = sb([128, 512], BF16)
        oT, BoT = sb([64, 512], F32)
        oc, Boc = sb([128, 4, 64], F32)
        sqj, Bsqj = sb([128, 4, 64], F32)
        sm, Bsm = sb([128, 4], F32)
        vs_, Bvs = sb([128, 4], F32)
        mixt = [sb([128, 256], BF16) for _ in range(2)]
        for ii, n in enumerate(range(nt - 1, -1, -1)):
            k_, Bk = kT[ii % 2]
            v_, Bv = vg[ii % 2]
            rows = slice(n * 128, (n + 1) * 128)
            dma("sp", k_, q.kc[n], [], [Bk])
            dma("sp", v_, q.vcg[rows, :], [], [Bv])
            cp("act", Rb, Rst, [BR], [BRb])
            for h2 in range(2):
                r_ = slice(h2 * 64, (h2 + 1) * 64)
                qh, Bqh = qz[h2][ii % 2]
                dma("sp", qh[r_], q.qc[n][r_], [], [Bqh])
                q3 = qh[r_].rearrange("p (a t) -> p a t", t=128)
                tt("dve", Qfz[h2][0][r_].rearrange("p (a t) -> p a t", t=128), q3, qdf[r_], ALU.mult,
                   [Bqh, Bqdf], [Qfz[h2][1]])
                tt("pool", Qbz[h2][0][r_].rearrange("p (a t) -> p a t", t=128), q3, qdb[r_], ALU.mult,
                   [Bqh, Bqdb], [Qbz[h2][1]])
            for h in range(4):
                pr, h2 = h // 2, h % 2
                qh, Bqh = qz[h2][ii % 2]
                mm(ps(2)[:, h * 128:(h + 1) * 128], k_[:, pr * 128:(pr + 1) * 128], qh[:, pr * 128:(pr + 1) * 128],
                   True, True, [Bk, Bqh], [PB[2]])
            tt("dve", PTm.rearrange("p (h t) -> p h t", t=128), ps(2).rearrange("p (h t) -> p h t", t=128), maskT,
               ALU.mult, [PB[2], Bmask], [BPTm])
            for h in range(4):
                pr, h2 = h // 2, h % 2
                r_ = slice(h2 * 64, (h2 + 1) * 64)
                po = ps(3)[0:64, h * 128:(h + 1) * 128]
                mm(po, v_[:, h * 64:(h + 1) * 64], PTm[:, h * 128:(h + 1) * 128], True, False, [Bv, BPTm], [PB[3]])
                mm(po, Sall[:, n, pr * 64:(pr + 1) * 64], Qfz[h2][0][:, pr * 128:(pr + 1) * 128], False, False,
                   [BSall, Qfz[h2][1]], [PB[3]])
                mm(po, Rb[:, pr * 64:(pr + 1) * 64], Qbz[h2][0][:, pr * 128:(pr + 1) * 128], False, True,
                   [BRb, Qbz[h2][1]], [PB[3]])
            R3 = Rst.rearrange("p (a e) -> p a e", e=64)
            tt("dve", R3, R3, bc(dCb.unsqueeze(2), [128, 2, 64]), ALU.mult, [BR, BdCb, BRb], [BR])
            tt("dve", Rst, Rst, kvbA[:, n, :], ALU.add, [BR, BkvbA], [BR])
            cp("act", oT, ps(3)[0:64, :], [PB[3]], [BoT])
            for h in range(4):
                tr(ps(4)[:, h * 64:(h + 1) * 64], oT[0:64, h * 128:(h + 1) * 128], ident_f[0:64, 0:64],
                   [BoT, Bidf], [PB[4]])
            pe3 = ps(4)[:, 0:256].rearrange("p (h e) -> p h e", e=64)
            rsum(sm, pe3, [PB[4]], [Bsm])
            ts("dve", sm, sm, 1.0 / 64, None, ALU.mult, None, [Bsm], [Bsm])
            tt("dve", oc, pe3, bc(sm.unsqueeze(2), [128, 4, 64]), ALU.subtract, [PB[4], Bsm], [Boc])
            act(sqj, oc, AF.Square, [Boc], [Bsqj])
            rsum(vs_, sqj, [Bsqj], [Bvs])
            act(vs_, vs_, AF.Sqrt, [Bvs, Beps], [Bvs], scale=1.0 / 64, bias=epsb)
            recip(vs_, vs_, [Bvs], [Bvs])
            tt("dve", oc, oc, bc(vs_.unsqueeze(2), [128, 4, 64]), ALU.mult, [Boc, Bvs], [Boc])
            ocf = oc.rearrange("p h e -> p (h e)")
            tt("dve", ocf, ocf, cgn, ALU.mult, [Boc, Bcgn], [Boc])
            m_, Bm = mixt[ii % 2]
            tt("dve", m_, ocf, v_[:, 256:512], ALU.mult, [Boc, Bv], [Bm])
            dma("pool", q.mix[rows, 768:1024], m_, [Bm], [])
        S.barrier()

    def phaseCD(l):
        st["off"] = PERSIST
        wout, Bwout = sb([128, KC, D], BF16)
        wxq, Bwxq = sb([128, KC, D], BF16)
        wxo, Bwxo = sb([128, KC, D], BF16)
        load_w(wout, Bwout, W["w_out"][l], KC)
        load_w(wxq, Bwxq, W["w_xq"][l], KC)
        load_w(wxo, Bwxo, W["w_xo"][l], KC)
        gcr, Bgcr = sb([128, D], F32)
        load_bc(gcr, Bgcr, W["g_cross"][l])
        mKT = [sb([128, KC, MEM], BF16) for _ in seqs]
        mV = [sb([128, 2, D], BF16) for _ in seqs]
        keep = st["off"]
        wxk, Bwxk = sb([128, KC, D], BF16)
        wxv, Bwxv = sb([128, KC, D], BF16)
        load_w(wxk, Bwxk, W["w_xk"][l], KC)
        load_w(wxv, Bwxv, W["w_xv"][l], KC)
        gme, Bgme = sb([128, D], F32)
        load_bc(gme, Bgme, W["g_mem"][l])
        mt_, Bmt = sb([128, D], F32)
        junk = sb([128, D], BF16)
        ss = sb([128, 1], F32)
        rstd = sb([128, 1], F32)
        scr = (junk[0], junk[1], ss[0], ss[1], rstd[0], rstd[1])
        hb, Bhb = sb([128, D], BF16)
        mT, BmT = sb([128, KC, MEM], BF16)
        for q in seqs:
            for c in range(2):
                dma("sp", mt_, q.mem[c * 128:(c + 1) * 128, :], [], [Bmt])
                rmsnorm_to_bf16(mt_, Bmt, gme, Bgme, hb, Bhb, scr)
                transpose8(hb, Bhb, mT[:, :, c * 128:(c + 1) * 128], BmT, 0, "act")
            K_, BK = mKT[q.i]
            V_, BV = mV[q.i]
            for f in range(KC):
                b = 1 + f % 2
                for k in range(KC):
                    mm(ps(b)[:, 0:MEM], wxk[:, k, f * 128:(f + 1) * 128], mT[:, k, :], k == 0, k == KC - 1,
                       [Bwxk, BmT], [PB[b]])
                cp("act", K_[:, f, :], ps(b)[:, 0:MEM], [PB[b]], [BK])
            for c in range(2):
                for hf in range(2):
                    b = 3 + hf
                    for k in range(KC):
                        mm(ps(b), mT[:, k, c * 128:(c + 1) * 128], wxv[:, k, hf * 512:(hf + 1) * 512], k == 0,
                           k == KC - 1, [BmT, Bwxv], [PB[b]])
                    cp("dve", V_[:, c, hf * 512:(hf + 1) * 512], ps(b), [PB[b]], [BV])
        S.barrier()
        st["off"] = keep
        G = 4
        xt = [sb([128, D], F32) for _ in range(G)]
        mx = [sb([128, D], BF16) for _ in range(2)]
        junk = sb([128, D], BF16)
        ss = sb([128, 1], F32)
        rstd = sb([128, 1], F32)
        scr = (junk[0], junk[1], ss[0], ss[1], rstd[0], rstd[1])
        hb, Bhb = sb([128, D], BF16)
        mixT, BmixT = sb([128, KC, 128], BF16)
        hT, BhT = sb([128, KC, 512], BF16)
        QT, BQT = sb([128, KC, 512], BF16)
        PT, BPT = sb([128, 8, 512], BF16)
        OT, BOT = sb([128, KC, 512], BF16)
        rden, Brden = sb([128, 512], F32)
        for q in seqs:
            src = q.x_in if l == 0 else q.xs
            K_, BK = mKT[q.i]
            V_, BV = mV[q.i]
            for gi in range(q.nt // G):
                for j in range(G):
                    t = gi * G + j
                    rows = slice(t * 128, (t + 1) * 128)
                    x_, Bx = xt[j]
                    m_, Bm = mx[j % 2]
                    dma("sp", x_, src[rows, :], [], [Bx])
                    dma("sp", m_, q.mix[rows, :], [], [Bm])
                    transpose8(m_, Bm, mixT, BmixT, 0, "act")
                    for hf in range(2):
                        for k in range(KC):
                            mm(ps(1 + hf), mixT[:, k, :], wout[:, k, hf * 512:(hf + 1) * 512], k == 0, k == KC - 1,
                               [BmixT, Bwout], [PB[1 + hf]])
                    tt("dve", x_, x_, ps(1, 2), ALU.add, [Bx, PB[1], PB[2]], [Bx])
                    rmsnorm_to_bf16(x_, Bx, gcr, Bgcr, hb, Bhb, scr)
                    transpose8(hb, Bhb, hT[:, :, j * 128:(j + 1) * 128], BhT, 0, "act")
                for f in range(KC):
                    b = 3 + f % 2
                    for k in range(KC):
                        mm(ps(b), wxq[:, k, f * 128:(f + 1) * 128], hT[:, k, :], k == 0, k == KC - 1,
                           [Bwxq, BhT], [PB[b]])
                    cp("act" if f % 2 else "dve", QT[:, f, :], ps(b), [PB[b]], [BQT])
                for h in range(4):
                    for mc in range(2):
                        b = 5 + mc
                        for dc in range(2):
                            mm(ps(b), K_[:, 2 * h + dc, mc * 128:(mc + 1) * 128], QT[:, 2 * h + dc, :], dc == 0,
                               dc == 1, [BK, BQT], [PB[b]])
                        act(PT[:, 2 * h + mc, :], ps(b), AF.Exp, [PB[b]], [BPT], scale=1.0 / 16)
                for h in range(4):
                    for mc in range(2):
                        mm(ps(7), ones_b, PT[:, 2 * h + mc, :], mc == 0, mc == 1, [Bones, BPT], [PB[7]])
                    recip(rden, ps(7), [PB[7]], [Brden])
                    for dc in range(2):
                        b = 3 + dc
                        f = 2 * h + dc
                        for mc in range(2):
                            mm(ps(b), V_[:, mc, f * 128:(f + 1) * 128], PT[:, 2 * h + mc, :], mc == 0, mc == 1,
                               [BV, BPT], [PB[b]])
                        tt("dve", OT[:, f, :], ps(b), rden, ALU.mult, [PB[b], Brden], [BOT])
                for j in range(G):
                    t = gi * G + j
                    x_, Bx = xt[j]
                    for hf in range(2):
                        for f in range(KC):
                            mm(ps(1 + hf), OT[:, f, j * 128:(j + 1) * 128], wxo[:, f, hf * 512:(hf + 1) * 512], f == 0,
                               f == KC - 1, [BOT, Bwxo], [PB[1 + hf]])
                    tt("dve", x_, x_, ps(1, 2), ALU.add, [Bx, PB[1], PB[2]], [Bx])
                    dma("pool", q.xs[t * 128:(t + 1) * 128, :], x_, [Bx], [])
        S.barrier()

    def phaseE(l):
        st["off"] = PERSIST
        last = l == DEPTH - 1
        wg, Bwg = sb([128, KC, DFF], BF16)
        wu, Bwu = sb([128, KC, DFF], BF16)
        wd, Bwd = sb([128, FC, D], BF16)
        load_w(wg, Bwg, W["w_gate"][l], KC)
        load_w(wu, Bwu, W["w_up"][l], KC)
        load_w(wd, Bwd, W["w_down"][l], FC)
        gff, Bgff = sb([128, D], F32)
        load_bc(gff, Bgff, W["g_ffn"][l])
        if last:
            gfi, Bgfi = sb([128, D], F32)
            load_bc(gfi, Bgfi, W["g_final"])
        G = 4
        xt = [sb([128, D], F32) for _ in range(2)]
        junk = sb([128, D], BF16)
        ss = sb([128, 1], F32)
        rstd = sb([128, 1], F32)
        scr = (junk[0], junk[1], ss[0], ss[1], rstd[0], rstd[1])
        hb, Bhb = sb([128, D], BF16)
        hT, BhT = sb([128, KC, 512], BF16)
        sg = [sb([128, 512], F32) for _ in range(2)]
        aT, BaT = sb([128, FC, 512], BF16)
        for q in seqs:
            for gi in range(q.nt // G):
                for j in range(G):
                    t = gi * G + j
                    x_, Bx = xt[j % 2]
                    dma("sp", x_, q.xs[t * 128:(t + 1) * 128, :], [], [Bx])
                    rmsnorm_to_bf16(x_, Bx, gff, Bgff, hb, Bhb, scr)
                    transpose8(hb, Bhb, hT[:, :, j * 128:(j + 1) * 128], BhT, 0, "act")
                for f in range(FC):
                    bg, bu = 1 + f % 2, 3 + f % 2
                    for k in range(KC):
                        mm(ps(bg), wg[:, k, f * 128:(f + 1) * 128], hT[:, k, :], k == 0, k == KC - 1,
                           [Bwg, BhT], [PB[bg]])
                    for k in range(KC):
                        mm(ps(bu), wu[:, k, f * 128:(f + 1) * 128], hT[:, k, :], k == 0, k == KC - 1,
                           [Bwu, BhT], [PB[bu]])
                    s_, Bs = sg[f % 2]
                    act(s_, ps(bg), AF.Silu, [PB[bg]], [Bs])
                    tt("dve", aT[:, f, :], s_, ps(bu), ALU.mult, [Bs, PB[bu]], [BaT])
                for j in range(G):
                    t = gi * G + j
                    x_, Bx = xt[j % 2]
                    dma("sp", x_, q.xs[t * 128:(t + 1) * 128, :], [], [Bx])
                    for hf in range(2):
                        for f in range(FC):
                            mm(ps(5 + hf), aT[:, f, j * 128:(j + 1) * 128], wd[:, f, hf * 512:(hf + 1) * 512], f == 0,
                               f == FC - 1, [BaT, Bwd], [PB[5 + hf]])
                    tt("dve", x_, x_, ps(5, 2), ALU.add, [Bx, PB[5], PB[6]], [Bx])
                    if last:
                        act(scr[0], x_, AF.Square, [Bx], [scr[1], scr[3]], accum_out=scr[2])
                        act(scr[2], scr[2], AF.Sqrt, [scr[3], Beps], [scr[3]], scale=1.0 / D, bias=epsb)
                        recip(scr[4], scr[2], [scr[3]], [scr[5]])
                        stt("dve", x_, x_, scr[4][:, 0:1], gfi, ALU.mult, ALU.mult, [Bx, scr[5], Bgfi], [Bx])
                        dma("pool", q.y[t * 128:(t + 1) * 128, :], x_, [Bx], [])
                    else:
                        dma("pool", q.xs[t * 128:(t + 1) * 128, :], x_, [Bx], [])
        S.barrier()

    setup()
    for l in range(DEPTH):
        phaseA(l)
        for q in seqs:
            attn(l, q, "a")
            attn(l, q, "b")
            retention(l, q)
        phaseCD(l)
        phaseE(l)
    S.emit()
    es.close()
    return nc


WNAMES = ("g_mix", "w_in", "a_q_norm", "a_k_norm", "a_out_norm", "b_sink", "b_out_norm", "c_decay_fwd",
          "c_decay_bwd", "c_gn", "w_out", "g_cross", "g_mem", "w_xq", "w_xk", "w_xv", "w_xo", "g_ffn", "w_gate",
          "w_up", "w_down", "g_final")


def make_in_maps(inputs, n_cores=8):
    xp, xsm = inputs["x_prompt"], inputs["x_sample"]
    mp, ms = inputs["mem_prompt"], inputs["mem_sample"]
    wts = {k: np.ascontiguousarray(np.asarray(inputs[k], dtype=np.float32)) for k in WNAMES}
    maps = []
    for c in range(n_cores):
        m = dict(wts)
        m["x0"] = np.ascontiguousarray(xp[c % xp.shape[0]], dtype=np.float32)
        m["m0"] = np.ascontiguousarray(mp[c % mp.shape[0]], dtype=np.float32)
        m["x1"] = np.ascontiguousarray(xsm[c % xsm.shape[0]], dtype=np.float32)
        m["m1"] = np.ascontiguousarray(ms[c % ms.shape[0]], dtype=np.float32)
        maps.append(m)
    return maps


def kernel(**inputs):
    xp, xsm = np.asarray(inputs["x_prompt"]), np.asarray(inputs["x_sample"])
    depth = np.asarray(inputs["w_in"]).shape[0]
    nc = build(xp.shape[1], xsm.shape[1], depth)
    maps = make_in_maps({k: np.asarray(v) for k, v in inputs.items()})
    res = run_bass_kernel_spmd(nc, maps, core_ids=list(range(8)))
    yp = np.stack([res.results[b]["y0"] for b in range(xp.shape[0])], axis=0).astype(np.float32)
    ysm = np.stack([res.results[b]["y1"] for b in range(xsm.shape[0])], axis=0).astype(np.float32)
    return (yp, ysm)
```

```python
import math
from contextlib import ExitStack

import numpy as np
import concourse.bass as bass
import concourse.mybir as mybir
from concourse.bass_utils import run_bass_kernel_spmd

F32 = mybir.dt.float32
BF16 = mybir.dt.bfloat16
I32 = mybir.dt.int32
AF = mybir.ActivationFunctionType
ALU = mybir.AluOpType
AX = mybir.AxisListType

D = 1024
KC = 8
DFF = 2816
FC = 22
MEM = 256
EPS = 1e-6
THETA = 10000.0
KRING = 8
ATTACH = True


class Buf:
    __slots__ = ("name", "w", "rs", "rd")

    def __init__(self, name):
        self.name = name
        self.w = None
        self.rs = {}
        self.rd = []


class Op:
    __slots__ = ("eng", "fn", "deps", "need", "val", "dma", "slot", "dval")


class Sched:
    ENG = ("pe", "act", "dve", "pool", "sp")

    def __init__(self, nc):
        self.nc = nc
        self.h = {"pe": nc.tensor, "act": nc.scalar, "dve": nc.vector, "pool": nc.gpsimd, "sp": nc.sync}
        self.ops = {e: [] for e in self.ENG}
        self.dmas = {e: [] for e in self.ENG}
        self.pending = []

    def op(self, eng, fn, r=(), w=(), dma=False):
        o = Op()
        o.eng, o.fn, o.dma, o.need, o.val = eng, fn, dma, False, 0
        deps = []
        for b in r:
            if b.w is not None:
                deps.append(b.w)
        for b in w:
            if b.w is not None and not b.rs and not b.rd:
                deps.append(b.w)
            deps.extend(b.rs.values())
            deps.extend(b.rd)
        if dma:
            n = len(self.dmas[eng])
            o.slot, o.dval = n % KRING, 16 * (n // KRING + 1)
            if n >= KRING:
                deps.append(self.dmas[eng][n - KRING])
            self.dmas[eng].append(o)
            self.pending.append(o)
        o.deps = [d for d in deps if d.dma or not (eng == "pe" and d.eng == "pe")]
        for d in o.deps:
            if not d.dma:
                d.need = True
        for b in r:
            if dma:
                b.rd.append(o)
            else:
                b.rs[eng] = o
        for b in w:
            b.w, b.rs, b.rd = o, {}, []
        self.ops[eng].append(o)
        return o

    def barrier(self):
        deps = list(self.pending)
        for e in self.ENG:
            for o in reversed(self.ops[e]):
                if o.fn is not None and not o.dma:
                    deps.append(o)
                    o.need = True
                    break
        self.pending = []
        for e in self.ENG:
            o = Op()
            o.eng, o.fn, o.dma, o.need, o.val, o.deps = e, None, False, False, 0, deps
            self.ops[e].append(o)

    def emit(self):
        nc = self.nc
        with ExitStack() as es:
            csem = {e: es.enter_context(nc.semaphore("c_" + e)) for e in self.ENG}
            dsem = {e: [es.enter_context(nc.semaphore("d_%s%d" % (e, i))) for i in range(KRING)]
                    for e in self.ENG if self.dmas[e]}
            for e in self.ENG:
                c = 0
                for o in self.ops[e]:
                    if o.need and not o.dma and o.fn is not None:
                        c += 1
                        o.val = c
            block = es.enter_context(nc.Block())
            names = {"pe": "tensor", "act": "scalar", "dve": "vector", "pool": "gpsimd", "sp": "sync"}

            def make(e):
                def body(h):
                    known = {}
                    for o in self.ops[e]:
                        waits = {}
                        for d in o.deps:
                            key, val = ((d.eng, d.slot), d.dval) if d.dma else ((d.eng, -1), d.val)
                            if known.get(key, 0) >= val:
                                continue
                            if waits.get(key, 0) < val:
                                waits[key] = val
                        wl = list(waits.items())
                        attach = None
                        if ATTACH and wl and o.fn is not None and not o.dma and e in ("act", "dve", "pool"):
                            attach = wl.pop()
                        for key, val in wl:
                            sem = dsem[key[0]][key[1]] if key[1] >= 0 else csem[key[0]]
                            h.wait_ge(sem, val)
                            known[key] = val
                        if o.fn is None:
                            continue
                        ins = o.fn(h)
                        if attach is not None:
                            key, val = attach
                            sem = dsem[key[0]][key[1]] if key[1] >= 0 else csem[key[0]]
                            ins._wait_ge(sem, val)
                            known[key] = val
                        if o.dma:
                            ins.then_inc(dsem[e][o.slot], 16)
                        elif o.need:
                            ins.then_inc(csem[e], 1)
                return body

            for e in self.ENG:
                getattr(block, names[e])(make(e))


class Seq:
    pass


def build(T1, T2, DEPTH):
    nc = bass.Bass("TRN2", target_bir_lowering=False)
    S = Sched(nc)
    es = ExitStack()

    def din(name, shape):
        return nc.dram_tensor(name, list(shape), F32, kind="ExternalInput").ap()

    def dscr(name, shape, dt):
        return nc.dram_tensor(name, list(shape), dt, kind="Internal").ap()

    L = DEPTH
    W = dict(
        g_mix=din("g_mix", (L, D)), w_in=din("w_in", (L, D, 2304)), a_q_norm=din("a_q_norm", (L, 64)),
        a_k_norm=din("a_k_norm", (L, 64)), a_out_norm=din("a_out_norm", (L, 512)), b_sink=din("b_sink", (L, 4)),
        b_out_norm=din("b_out_norm", (L, 256)), c_decay_fwd=din("c_decay_fwd", (L, 4)),
        c_decay_bwd=din("c_decay_bwd", (L, 4)), c_gn=din("c_gn", (L, 256)), w_out=din("w_out", (L, D, D)),
        g_cross=din("g_cross", (L, D)), g_mem=din("g_mem", (L, D)), w_xq=din("w_xq", (L, D, D)),
        w_xk=din("w_xk", (L, D, D)), w_xv=din("w_xv", (L, D, D)), w_xo=din("w_xo", (L, D, D)),
        g_ffn=din("g_ffn", (L, D)), w_gate=din("w_gate", (L, D, DFF)), w_up=din("w_up", (L, D, DFF)),
        w_down=din("w_down", (L, DFF, D)), g_final=din("g_final", (D,)),
    )
    seqs = []
    for si, T in enumerate((T1, T2)):
        q = Seq()
        q.T, q.nt, q.i = T, T // 128, si
        q.x_in = din("x%d" % si, (T, D))
        q.mem = din("m%d" % si, (MEM, D))
        q.y = nc.dram_tensor("y%d" % si, [T, D], F32, kind="ExternalOutput").ap()
        n = q.nt
        q.xs = dscr("xs%d" % si, (T, D), F32)
        q.qa = dscr("qa%d" % si, (n, 128, 512), BF16)
        q.ka = dscr("ka%d" % si, (n, 128, 128), BF16)
        q.va = dscr("va%d" % si, (T, 130), BF16)
        q.qb = dscr("qb%d" % si, (n, 128, 256), BF16)
        q.kb = dscr("kb%d" % si, (n, 128, 128), BF16)
        q.vb = dscr("vb%d" % si, (T, 130), BF16)
        q.qc = dscr("qc%d" % si, (n, 128, 256), BF16)
        q.kc = dscr("kc%d" % si, (n, 128, 256), BF16)
        q.kct = dscr("kct%d" % si, (T, 256), BF16)
        q.vcg = dscr("vcg%d" % si, (T, 512), BF16)
        q.mix = dscr("mix%d" % si, (T, D), BF16)
        seqs.append(q)
    NTM = max(q.nt for q in seqs)
    tab_s = dscr("tab_s", (NTM, 128, 256), F32)

    ARENA = 98000
    arena = es.enter_context(nc.sbuf_tensor("arena", [128, ARENA], BF16))
    psum = es.enter_context(nc.psum_tensor("psum", [128, 4096], F32))
    PB = [Buf("psb%d" % i) for i in range(8)]
    st = {"off": 0, "n": 0}

    def sb(shape, dt, name=None):
        n = int(np.prod(shape[1:]))
        nb = n * (2 if dt in (F32, I32) else 1)
        nb_al = (nb + 15) // 16 * 16
        off = st["off"]
        assert off + nb_al <= ARENA, "SBUF arena overflow %d" % (off + nb_al)
        st["off"] = off + nb_al
        v = arena[0:shape[0], off:off + nb]
        if dt != BF16:
            v = v.bitcast(dt)
        if len(shape) == 3:
            v = v.rearrange("p (a b) -> p a b", b=shape[2])
        elif len(shape) == 4:
            v = v.rearrange("p (a b c) -> p a b c", b=shape[2], c=shape[3])
        st["n"] += 1
        return v, Buf(name or "t%d" % st["n"])

    def ps(bank, nb=1, dt=F32):
        v = psum[:, bank * 512:(bank + nb) * 512]
        return v.bitcast(dt) if dt != F32 else v

    def mm(out, lhsT, rhs, start, stop, r, w):
        return S.op("pe", lambda e: e.matmul(out, lhsT=lhsT, rhs=rhs, start=start, stop=stop), r, w)

    def tr(out, in_, ident, r, w):
        return S.op("pe", lambda e: e.transpose(out=out, in_=in_, identity=ident), r, w)

    def act(out, in_, func, r, w, **kw):
        return S.op("act", lambda e: e.activation(out=out, in_=in_, func=func, **kw), r, w)

    def tt(eng, out, in0, in1, op, r, w):
        return S.op(eng, lambda e: e.tensor_tensor(out=out, in0=in0, in1=in1, op=op), r, w)

    def ts(eng, out, in0, s1, s2, op0, op1, r, w):
        if op1 is None:
            return S.op(eng, lambda e: e.tensor_scalar(out=out, in0=in0, scalar1=s1, scalar2=None, op0=op0), r, w)
        return S.op(eng, lambda e: e.tensor_scalar(out=out, in0=in0, scalar1=s1, scalar2=s2, op0=op0, op1=op1), r, w)

    def stt(eng, out, in0, scalar, in1, op0, op1, r, w):
        return S.op(eng, lambda e: e.scalar_tensor_tensor(out=out, in0=in0, scalar=scalar, in1=in1, op0=op0, op1=op1), r, w)

    def cp(eng, out, in_, r, w):
        if eng == "act":
            return act(out, in_, AF.Copy, r, w)
        return S.op(eng, lambda e: e.tensor_copy(out=out, in_=in_), r, w)

    def recip(out, in_, r, w):
        return S.op("dve", lambda e: e.reciprocal(out=out, in_=in_), r, w)

    def rsum(out, in_, r, w):
        return S.op("dve", lambda e: e.reduce_sum(out=out, in_=in_, axis=AX.X), r, w)

    def memset(eng, out, val, w):
        return S.op(eng, lambda e: e.memset(out, val), (), w)

    def iota(out, pattern, base, cm, w):
        return S.op("pool", lambda e: e.iota(out, pattern=pattern, base=base, channel_multiplier=cm,
                                             allow_small_or_imprecise_dtypes=True), (), w)

    def dma(q, out, in_, r, w):
        return S.op(q, lambda e: e.dma_start(out=out, in_=in_), r, w, dma=True)

    def bc(ap, shape):
        return ap.to_broadcast(list(shape))

    ident_b, Bidb = sb([128, 128], BF16, "ident_b")
    ident_f, Bidf = sb([128, 128], F32, "ident_f")
    ones_b, Bones = sb([128, 128], BF16, "ones_b")
    epsb, Beps = sb([128, 1], F32, "eps")
    triL, BtriL = sb([128, 128], BF16, "triL")
    triR, BtriR = sb([128, 128], BF16, "triR")
    posD, BposD = sb([128, 128], F32, "posD")
    negD, BnegD = sb([128, 128], F32, "negD")
    aidx1, Ba1 = sb([128, 128], F32, "aidx1")
    aidxC, BaC = sb([128, 128], F32, "aidxC")
    cidx, Bcidx = sb([128, 1], F32, "cidx")
    c127, Bc127 = sb([128, 1], F32, "c127")
    PERSIST = st["off"]

    def setup():
        Dm, BD = sb([128, 128], F32, "Dm")
        iota(Dm, [[1, 128]], 0, -1, [BD])
        ts("dve", ident_f, Dm, 0.0, None, ALU.is_equal, None, [BD], [Bidf])
        ts("dve", ident_b, Dm, 0.0, None, ALU.is_equal, None, [BD], [Bidb])
        ts("dve", triL, Dm, 0.0, None, ALU.is_le, None, [BD], [BtriL])
        ts("dve", triR, Dm, 0.0, None, ALU.is_ge, None, [BD], [BtriR])
        ts("dve", posD, Dm, 0.0, None, ALU.max, None, [BD], [BposD])
        ts("dve", negD, Dm, -1.0, 0.0, ALU.mult, ALU.max, [BD], [BnegD])
        memset("pool", ones_b, 1.0, [Bones])
        memset("pool", epsb, EPS, [Beps])
        iota(aidx1, [[1, 128]], 1, 0, [Ba1])
        iota(aidxC, [[-1, 128]], 128, 0, [BaC])
        iota(cidx, [[1, 1]], 0, 1, [Bcidx])
        iota(c127, [[1, 1]], 127, -1, [Bc127])
        NT = NTM
        pos, Bpos = sb([128, NT], F32)
        iota(pos, [[128, NT]], 0, 1, [Bpos])
        hi, Bhi = sb([128, 1], F32)
        ts("dve", hi, cidx, 64.0, None, ALU.is_ge, None, [Bcidx], [Bhi])
        colv, Bcol = sb([128, 1], F32)
        stt("dve", colv, hi, -64.0, cidx, ALU.mult, ALU.add, [Bhi, Bcidx], [Bcol])
        rowv, Brow = sb([128, NT], F32)
        iota(rowv, [[2, NT]], 0, 0, [Brow])
        ts("dve", rowv, rowv, hi[:, 0:1], None, ALU.add, None, [Brow, Bhi], [Brow])
        i32t, Bi32 = sb([128, 32], F32)
        iota(i32t, [[1, 32]], 0, 0, [Bi32])
        inv32, Binv32 = sb([128, 32], F32)
        act(inv32, i32t, AF.Exp, [Bi32], [Binv32], scale=-math.log(THETA) / 32.0)
        inv16, Binv16 = sb([128, 16], F32)
        act(inv16, i32t[:, 0:16], AF.Exp, [Bi32], [Binv16], scale=-math.log(THETA) / 16.0)
        angs, Bangs = sb([128, NT, 32], F32)
        tt("dve", angs, bc(pos.unsqueeze(2), [128, NT, 32]), bc(inv32.unsqueeze(1), [128, NT, 32]), ALU.mult,
           [Bpos, Binv32], [Bangs])
        anga, Banga = sb([128, NT, 32], F32)
        tt("dve", anga[:, :, 0:16], bc(rowv.unsqueeze(2), [128, NT, 16]), bc(inv16.unsqueeze(1), [128, NT, 16]),
           ALU.mult, [Brow, Binv16], [Banga])
        ts("dve", anga[:, :, 16:32], bc(inv16.unsqueeze(1), [128, NT, 16]), colv[:, 0:1], None, ALU.mult, None,
           [Binv16, Bcol, Banga], [Banga])
        tab, Btab = sb([128, NT, 256], F32)
        u, Bu = sb([128, NT, 32], F32)
        ni, Bni = sb([128, NT, 32], I32)
        nf, Bnf = sb([128, NT, 32], F32)
        sn, Bsn = sb([128, NT, 32], F32)

        def sinshift(ang, Bang, shift):
            ts("dve", u, ang, 1.0 / (2 * math.pi), 0.5 + shift / (2 * math.pi), ALU.mult, ALU.add, [Bang], [Bu])
            cp("dve", ni, u, [Bu], [Bni])
            cp("dve", nf, ni, [Bni], [Bnf])
            tt("dve", u, u, nf, ALU.subtract, [Bu, Bnf], [Bu])
            ts("dve", nf, u, 0.0, None, ALU.is_lt, None, [Bu], [Bnf])
            tt("dve", u, u, nf, ALU.add, [Bu, Bnf], [Bu])
            ts("dve", nf, u, 1.0, None, ALU.is_ge, None, [Bu], [Bnf])
            tt("dve", u, u, nf, ALU.subtract, [Bu, Bnf], [Bu])
            ts("dve", u, u, 2 * math.pi, -math.pi, ALU.mult, ALU.add, [Bu], [Bu])
            ts("dve", u, u, -3.14159, 3.14159, ALU.max, ALU.min, [Bu], [Bu])
            act(sn, u, AF.Sin, [Bu], [Bsn])

        for base, ang, Bang in ((0, anga, Banga), (128, angs, Bangs)):
            sinshift(ang, Bang, math.pi / 2)
            cp("dve", tab[:, :, base:base + 32], sn, [Bsn], [Btab])
            cp("dve", tab[:, :, base + 32:base + 64], sn, [Bsn], [Btab])
            sinshift(ang, Bang, 0.0)
            ts("dve", tab[:, :, base + 64:base + 96], sn, -1.0, None, ALU.mult, None, [Bsn], [Btab])
            cp("dve", tab[:, :, base + 96:base + 128], sn, [Bsn], [Btab])
        dma("sp", tab_s.rearrange("t p c -> p t c"), tab, [Btab], [])
        S.barrier()

    def rmsnorm_to_bf16(xt, Bx, gain, Bg, hb, Bhb, scr):
        junk, Bj, ss, Bss, rstd, Brs = scr
        act(junk, xt, AF.Square, [Bx], [Bj, Bss], accum_out=ss)
        act(ss, ss, AF.Sqrt, [Bss, Beps], [Bss], scale=1.0 / D, bias=epsb)
        recip(rstd, ss, [Bss], [Brs])
        stt("dve", hb, xt, rstd[:, 0:1], gain, ALU.mult, ALU.mult, [Bx, Brs, Bg], [Bhb])

    def transpose8(src, Bsrc, dst, Bdst, bank, cpeng):
        pt = ps(bank, 1, BF16)
        for k in range(KC):
            tr(pt[:, k * 128:(k + 1) * 128], src[:, k * 128:(k + 1) * 128], ident_b, [Bsrc, Bidb], [PB[bank]])
        cp(cpeng, dst, pt.rearrange("p (k t) -> p k t", t=128), [PB[bank]], [Bdst])

    def load_w(dst, Bdst, src2d, nk):
        dma("pool", dst, src2d.rearrange("(k p) n -> p k n", p=128), [], [Bdst])

    def load_bc(dst, Bdst, src1d):
        dma("sp", dst, src1d.partition_broadcast(128), [], [Bdst])

    def phaseA(l):
        st["off"] = PERSIST
        win, Bwin = sb([128, KC, 2304], BF16, "win")
        wi = W["w_in"][l]
        for k in range(KC):
            rows = wi[k * 128:(k + 1) * 128, :]
            for j in range(4):
                dma("pool", win[:, k, j * 128:(j + 1) * 128].rearrange("p (g d) -> p g d", g=2),
                    rows[:, 0:512].rearrange("p (g j d) -> p j g d", g=2, j=4)[:, j, :, :], [], [Bwin])
            for j in range(2):
                dma("pool", win[:, k, 640 + j * 128:640 + (j + 1) * 128].rearrange("p (g d) -> p g d", g=2),
                    rows[:, 768:1024].rearrange("p (g j d) -> p j g d", g=2, j=2)[:, j, :, :], [], [Bwin])
        for (d0, d1, s0) in ((512, 640, 512), (896, 1024, 1024), (1024, 1536, 1280), (1536, 1664, 640),
                             (1664, 1792, 1152), (1792, 2304, 1792)):
            dma("pool", win[:, :, d0:d1], wi[:, s0:s0 + (d1 - d0)].rearrange("(k p) n -> p k n", p=128), [], [Bwin])
        gmix, Bgmix = sb([128, D], F32)
        load_bc(gmix, Bgmix, W["g_mix"][l])
        gqk, Bgqk = sb([128, 10, 64], F32)
        dma("sp", gqk[:, 0:8, :], bc(W["a_q_norm"][l].partition_broadcast(128).unsqueeze(1), [128, 8, 64]), [], [Bgqk])
        dma("sp", gqk[:, 8:10, :], bc(W["a_k_norm"][l].partition_broadcast(128).unsqueeze(1), [128, 2, 64]), [], [Bgqk])
        xt = [sb([128, D], F32) for _ in range(2)]
        tabt = [sb([128, 256], F32) for _ in range(2)]
        junk = sb([128, D], BF16)
        ss = sb([128, 1], F32)
        rstd = sb([128, 1], F32)
        scr = (junk[0], junk[1], ss[0], ss[1], rstd[0], rstd[1])
        hb, Bhb = sb([128, D], BF16)
        hT, BhT = sb([128, KC, 128], BF16)
        pfs = [sb([128, 2304], F32) for _ in range(2)]
        sq, Bsq = sb([128, 640], F32)
        ssh, Bssh = sb([128, 10], F32)
        rsh, Brsh = sb([128, 10], F32)
        ra, Bra = sb([128, 896], F32)
        rb, Brb = sb([128, 896], F32)
        pb = [sb([128, 1536], BF16) for _ in range(2)]
        vaa = [sb([128, 2, 65], BF16) for _ in range(2)]
        vab = [sb([128, 2, 65], BF16) for _ in range(2)]
        vcg = [sb([128, 512], BF16) for _ in range(2)]
        pT = [sb([128, 12, 128], BF16) for _ in range(2)]
        for v_, B_ in vaa + vab:
            memset("pool", v_, 1.0, [B_])
        tiles = [(q, t) for q in seqs for t in range(q.nt)]

        def front(i):
                q, t = tiles[i]
                s = i % 2
                src = q.x_in if l == 0 else q.xs
                pf, Bpf = pfs[s]
                x_, Bx = xt[s]
                tb, Btb = tabt[s]
                dma("sp", x_, src[t * 128:(t + 1) * 128, :], [], [Bx])
                dma("sp", tb, tab_s[t], [], [Btb])
                rmsnorm_to_bf16(x_, Bx, gmix, Bgmix, hb, Bhb, scr)
                transpose8(hb, Bhb, hT, BhT, 0, "act")
                for cg in range(5):
                    c0 = cg * 512
                    cw = 512 if cg < 4 else 256
                    for k in range(KC):
                        mm(ps(1 + cg)[:, 0:cw], hT[:, k, :], win[:, k, c0:c0 + cw], k == 0, k == KC - 1,
                           [BhT, Bwin], [PB[1 + cg]])
                    cp("act", pf[:, c0:c0 + cw], ps(1 + cg)[:, 0:cw], [PB[1 + cg]], [Bpf])

        def back(i):
                q, t = tiles[i]
                s = i % 2
                pf, Bpf = pfs[s]
                tb, Btb = tabt[s]
                act(sq, pf[:, 0:640], AF.Square, [Bpf], [Bsq])
                rsum(ssh, sq.rearrange("p (h d) -> p h d", d=64), [Bsq], [Bssh])
                act(ssh, ssh, AF.Sqrt, [Bssh, Beps], [Bssh], scale=1.0 / 64, bias=epsb)
                recip(rsh, ssh, [Bssh], [Brsh])
                pf3 = pf[:, 0:640].rearrange("p (h d) -> p h d", d=64)
                tt("dve", pf3, pf3, bc(rsh.unsqueeze(2), [128, 10, 64]), ALU.mult, [Bpf, Brsh], [Bpf])
                tt("dve", pf3, pf3, gqk, ALU.mult, [Bpf, Bgqk], [Bpf])
                p_, Bp = pb[s]
                for (c0, nh, tb0) in ((0, 10, 0), (640, 14, 128)):
                    xv = pf[:, c0:c0 + nh * 64].rearrange("p (h d) -> p h d", d=64)
                    av = ra[:, 0:nh * 64].rearrange("p (h d) -> p h d", d=64)
                    bv = rb[:, 0:nh * 64].rearrange("p (h d) -> p h d", d=64)
                    CC = bc(tb[:, tb0:tb0 + 64].unsqueeze(1), [128, nh, 64])
                    nS = bc(tb[:, tb0 + 64:tb0 + 96].unsqueeze(1), [128, nh, 32])
                    pS = bc(tb[:, tb0 + 96:tb0 + 128].unsqueeze(1), [128, nh, 32])
                    tt("dve", av, xv, CC, ALU.mult, [Bpf, Btb], [Bra])
                    tt("pool", bv[:, :, 0:32], xv[:, :, 32:64], nS, ALU.mult, [Bpf, Btb], [Brb])
                    tt("pool", bv[:, :, 32:64], xv[:, :, 0:32], pS, ALU.mult, [Bpf, Btb, Brb], [Brb])
                    tt("dve", p_[:, c0:c0 + nh * 64].rearrange("p (h d) -> p h d", d=64), av, bv, ALU.add,
                       [Bra, Brb], [Bp])
                va_, Bva = vaa[s]
                vb_, Bvb = vab[s]
                vc_, Bvc = vcg[s]
                cp("act", va_[:, :, 0:64], pf[:, 1536:1664].rearrange("p (g d) -> p g d", d=64), [Bpf], [Bva])
                cp("act", vb_[:, :, 0:64], pf[:, 1664:1792].rearrange("p (g d) -> p g d", d=64), [Bpf], [Bvb])
                cp("act", vc_[:, 0:256], pf[:, 1792:2048], [Bpf], [Bvc])
                act(vc_[:, 256:512], pf[:, 2048:2304], AF.Silu, [Bpf], [Bvc])
                pt2 = ps(6, 2, BF16)
                T_, BT = pT[s]
                for j in range(12):
                    bank = 6 if j < 8 else 7
                    tr(pt2[:, j * 128:(j + 1) * 128], p_[:, j * 128:(j + 1) * 128], ident_b, [Bp, Bidb], [PB[bank]])
                cp("dve", T_[:, 0:8, :], pt2[:, 0:1024].rearrange("p (k t) -> p k t", t=128), [PB[6]], [BT])
                cp("dve", T_[:, 8:12, :], pt2[:, 1024:1536].rearrange("p (k t) -> p k t", t=128), [PB[7]], [BT])
                dma("pool", q.qa[t].rearrange("p (j t) -> p j t", t=128), T_[:, 0:4, :], [BT], [])
                dma("pool", q.ka[t], T_[:, 4, :], [BT], [])
                dma("pool", q.qb[t].rearrange("p (j t) -> p j t", t=128), T_[:, 5:7, :], [BT], [])
                dma("pool", q.kb[t], T_[:, 7, :], [BT], [])
                dma("pool", q.qc[t].rearrange("p (j t) -> p j t", t=128), T_[:, 8:10, :], [BT], [])
                dma("pool", q.kc[t].rearrange("p (j t) -> p j t", t=128), T_[:, 10:12, :], [BT], [])
                rows = slice(t * 128, (t + 1) * 128)
                dma("pool", q.kct[rows, :], p_[:, 1280:1536], [Bp], [])
                dma("pool", q.va[rows, :], va_.rearrange("p g d -> p (g d)"), [Bva], [])
                dma("pool", q.vb[rows, :], vb_.rearrange("p g d -> p (g d)"), [Bvb], [])
                dma("pool", q.vcg[rows, :], vc_, [Bvc], [])

        front(0)
        for i in range(len(tiles)):
            if i + 1 < len(tiles):
                front(i + 1)
            back(i)
        S.barrier()

    def attn(l, q, kind):
        st["off"] = PERSIST
        nt = q.nt
        nj = 4 if kind == "a" else 2
        qw = nj * 128
        H = 2 * nj
        qs, ks, vs = (q.qa, q.ka, q.va) if kind == "a" else (q.qb, q.kb, q.vb)
        col0 = 0 if kind == "a" else 512
        KT0, BKT0 = sb([128, nt, 128], BF16)
        KT1, BKT1 = sb([128, nt, 128], BF16)
        VA, BVA = sb([128, nt, 130], BF16)
        memset("pool", KT0[64:128], 0.0, [BKT0])
        memset("pool", KT1[0:64], 0.0, [BKT1])
        ksr = ks.rearrange("t p c -> p t c")
        dma("sp", KT0[0:64], ksr[0:64], [], [BKT0])
        dma("sp", KT1[64:128], ksr[64:128], [], [BKT1])
        dma("sp", VA, vs.rearrange("(t p) c -> p t c", p=128), [], [BVA])
        gout, Bgout = sb([128, H * 64], F32)
        load_bc(gout, Bgout, (W["a_out_norm"] if kind == "a" else W["b_out_norm"])[l])
        if kind == "b":
            esk, Besk = sb([128, 4], F32)
            load_bc(esk, Besk, W["b_sink"][l])
            act(esk, esk, AF.Exp, [Besk], [Besk])
        QT = [sb([128, qw], BF16) for _ in range(2)]
        PT = [sb([128, 2 * qw], BF16) for _ in range(3)]
        oT, BoT = sb([65, 2, qw], F32)
        o_, Bo = sb([128, H, 64], F32)
        den, Bden = sb([128, H], F32)
        junk, Bj = sb([128, H * 64], BF16)
        ss, Bss = sb([128, 1], F32)
        rstd, Brs = sb([128, 1], F32)
        mixt = [sb([128, H * 64], BF16) for _ in range(2)]
        nsb = 2 if kind == "a" else 1
        steps = []
        for i in range(nt):
            klist = list(range(nt)) if kind == "a" else [c for c in (i - 1, i, i + 1) if 0 <= c < nt]
            for idx, c in enumerate(klist):
                steps.append((i, idx, c, idx == len(klist) - 1))

        def emit_S(n):
            i, idx, c, lastc = steps[n]
            Q_, BQ = QT[i % 2]
            if idx == 0:
                dma("sp", Q_, qs[i], [], [BQ])
            b0 = (n % 2) * nsb
            pss = ps(b0, nsb)
            banks = [PB[b0 + x] for x in range(nsb)]
            mm(pss[:, 0:qw], KT0[:, c, :], Q_, True, True, [BKT0, BQ], banks)
            mm(pss[:, qw:2 * qw], KT1[:, c, :], Q_, True, True, [BKT1, BQ], banks)

        def emit_rest(n):
            i, idx, c, lastc = steps[n]
            b0 = (n % 2) * nsb
            pss = ps(b0, nsb)
            banks = [PB[b0 + x] for x in range(nsb)]
            P_, BP = PT[n % 3]
            act(P_, pss[:, 0:2 * qw], AF.Exp, banks, [BP], scale=0.125)
            if kind == "b" and c != i:
                tri = triL if c < i else triR
                Btri = BtriL if c < i else BtriR
                P3 = P_.rearrange("p (h t) -> p h t", t=128)
                tt("dve", P3, P3, bc(tri.unsqueeze(1), [128, 2 * nj, 128]), ALU.mult, [BP, Btri], [BP])
            for g in range(2):
                mm(ps(4 + g)[0:65, 0:qw], VA[:, c, g * 65:(g + 1) * 65], P_[:, g * qw:(g + 1) * qw],
                   idx == 0, lastc, [BVA, BP], [PB[4 + g]])
            while pend and (lastc or n - pend[0][1] >= 2):
                epilogue2(pend.pop(0)[0])
            if lastc:
                epilogue(i)
                pend.append((i, n))

        pend = []

        def epilogue(i):
            for g in range(2):
                cp("dve", oT[:, g, :], ps(4 + g)[0:65, 0:qw], [PB[4 + g]], [BoT])
            for g in range(2):
                for j in range(nj):
                    tr(ps(6 + g)[:, j * 65:(j + 1) * 65], oT[0:65, g, j * 128:(j + 1) * 128], ident_f[0:65, 0:65],
                       [BoT, Bidf], [PB[6 + g]])
            for g in range(2):
                pe3 = ps(6 + g)[:, 0:nj * 65].rearrange("p (j c) -> p j c", c=65)
                dg = den[:, g * nj:(g + 1) * nj]
                if kind == "b":
                    tt("dve", dg, pe3[:, :, 64], esk[:, g * nj:(g + 1) * nj], ALU.add, [PB[6 + g], Besk], [Bden])
                    recip(dg, dg, [Bden], [Bden])
                else:
                    recip(dg, pe3[:, :, 64], [PB[6 + g]], [Bden])
                tt("dve", o_[:, g * nj:(g + 1) * nj, :], pe3[:, :, 0:64], bc(dg.unsqueeze(2), [128, nj, 64]), ALU.mult,
                   [PB[6 + g], Bden], [Bo])

        def epilogue2(i):
            of = o_.rearrange("p h d -> p (h d)")
            act(junk, of, AF.Square, [Bo], [Bj, Bss], accum_out=ss)
            act(ss, ss, AF.Sqrt, [Bss, Beps], [Bss], scale=1.0 / (H * 64), bias=epsb)
            recip(rstd, ss, [Bss], [Brs])
            m_, Bm = mixt[i % 2]
            stt("dve", m_, of, rstd[:, 0:1], gout, ALU.mult, ALU.mult, [Bo, Brs, Bgout], [Bm])
            dma("pool", q.mix[i * 128:(i + 1) * 128, col0:col0 + H * 64], m_, [Bm], [])

        emit_S(0)
        for n in range(len(steps)):
            if n + 1 < len(steps):
                emit_S(n + 1)
            emit_rest(n)
        while pend:
            epilogue2(pend.pop(0)[0])
        S.barrier()

    def retention(l, q):
        st["off"] = PERSIST
        nt = q.nt
        lgf, Blgf = sb([128, 4], F32)
        lgb, Blgb = sb([128, 4], F32)
        load_bc(lgf, Blgf, W["c_decay_fwd"][l])
        load_bc(lgb, Blgb, W["c_decay_bwd"][l])
        for t_, B_ in ((lgf, Blgf), (lgb, Blgb)):
            act(t_, t_, AF.Exp, [B_], [B_])
            ts("dve", t_, t_, -1.0, None, ALU.mult, None, [B_], [B_])
        lpf, Blpf = sb([128, 2], F32)
        lpb, Blpb = sb([128, 2], F32)
        for (dst, Bd, srcv, Bs) in ((lpf, Blpf, lgf, Blgf), (lpb, Blpb, lgb, Blgb)):
            s3 = srcv.rearrange("p (a b) -> p a b", b=2)
            cp("dve", dst[0:64, :], s3[0:64, :, 0], [Bs], [Bd])
            cp("dve", dst[64:128, :], s3[64:128, :, 1], [Bs, Bd], [Bd])
        t1, Bt1 = sb([128, 128], F32)
        t2, Bt2 = sb([128, 128], F32)
        maskT, Bmask = sb([128, 4, 128], BF16)
        for h in range(4):
            ts("dve", t1, posD, lgf[:, h:h + 1], None, ALU.mult, None, [BposD, Blgf], [Bt1])
            stt("dve", t2, negD, lgb[:, h:h + 1], t1, ALU.mult, ALU.add, [BnegD, Blgb, Bt1], [Bt2])
            act(t2, t2, AF.Exp, [Bt2], [Bt2])
            ts("dve", maskT[:, h, :], t2, 0.125, None, ALU.mult, None, [Bt2], [Bmask])
        qdf, Bqdf = sb([128, 2, 128], BF16)
        qdb, Bqdb = sb([128, 2, 128], BF16)
        for pr in range(2):
            act(t1, aidx1, AF.Exp, [Ba1, Blpf], [Bt1], scale=lpf[:, pr:pr + 1])
            ts("dve", qdf[:, pr, :], t1, 0.125, None, ALU.mult, None, [Bt1], [Bqdf])
            act(t2, aidxC, AF.Exp, [BaC, Blpb], [Bt2], scale=lpb[:, pr:pr + 1])
            ts("dve", qdb[:, pr, :], t2, 0.125, None, ALU.mult, None, [Bt2], [Bqdb])
        kdf, Bkdf = sb([128, 4], F32)
        kdb, Bkdb = sb([128, 4], F32)
        ts("dve", kdf, lgf, c127[:, 0:1], None, ALU.mult, None, [Blgf, Bc127], [Bkdf])
        act(kdf, kdf, AF.Exp, [Bkdf], [Bkdf])
        ts("dve", kdb, lgb, cidx[:, 0:1], None, ALU.mult, None, [Blgb, Bcidx], [Bkdb])
        act(kdb, kdb, AF.Exp, [Bkdb], [Bkdb])
        dCf, BdCf = sb([128, 2], F32)
        dCb, BdCb = sb([128, 2], F32)
        act(dCf, lpf, AF.Exp, [Blpf], [BdCf], scale=128.0)
        act(dCb, lpb, AF.Exp, [Blpb], [BdCb], scale=128.0)
        cgn, Bcgn = sb([128, 256], F32)
        load_bc(cgn, Bcgn, W["c_gn"][l])
        Sall, BSall = sb([128, nt, 128], BF16)
        kvbA, BkvbA = sb([128, nt, 128], F32)
        Sst, BS = sb([128, 128], F32)
        Rst, BR = sb([128, 128], F32)
        Rb, BRb = sb([128, 128], BF16)
        memset("pool", Sst, 0.0, [BS])
        memset("pool", Rst, 0.0, [BR])
        kt = [sb([128, 256], BF16) for _ in range(2)]
        vt = [sb([128, 512], BF16) for _ in range(2)]
        kwf, Bkwf = sb([128, 256], BF16)
        kwb, Bkwb = sb([128, 256], BF16)
        for n in range(nt):
            k_, Bk = kt[n % 2]
            v_, Bv = vt[n % 2]
            rows = slice(n * 128, (n + 1) * 128)
            dma("sp", k_, q.kct[rows, :], [], [Bk])
            dma("sp", v_, q.vcg[rows, :], [], [Bv])
            k3 = k_.rearrange("p (h d) -> p h d", d=64)
            tt("dve", kwf.rearrange("p (h d) -> p h d", d=64), k3, bc(kdf.unsqueeze(2), [128, 4, 64]), ALU.mult,
               [Bk, Bkdf], [Bkwf])
            tt("pool", kwb.rearrange("p (h d) -> p h d", d=64), k3, bc(kdb.unsqueeze(2), [128, 4, 64]), ALU.mult,
               [Bk, Bkdb], [Bkwb])
            for pr in range(2):
                mm(ps(0)[:, pr * 256:(pr + 1) * 256], kwf[:, pr * 128:(pr + 1) * 128], v_[:, 0:256], True, True,
                   [Bkwf, Bv], [PB[0]])
                mm(ps(1)[:, pr * 256:(pr + 1) * 256], kwb[:, pr * 128:(pr + 1) * 128], v_[:, 0:256], True, True,
                   [Bkwb, Bv], [PB[1]])
            cp("act", Sall[:, n, :], Sst, [BS], [BSall])
            S3 = Sst.rearrange("p (a e) -> p a e", e=64)
            tt("dve", S3, S3, bc(dCf.unsqueeze(2), [128, 2, 64]), ALU.mult, [BS, BdCf, BSall], [BS])
            for h2 in range(2):
                r_ = slice(h2 * 64, (h2 + 1) * 64)
                blkf = psum[r_, h2 * 64:h2 * 64 + 768].rearrange("p (a x) -> p a x", x=384)[:, :, 0:64]
                blkb = psum[r_, 512 + h2 * 64:512 + h2 * 64 + 768].rearrange("p (a x) -> p a x", x=384)[:, :, 0:64]
                tt("dve", S3[r_], S3[r_], blkf, ALU.add, [BS, PB[0]], [BS])
                cp("act", kvbA[r_, n, :].rearrange("p (a e) -> p a e", e=64), blkb, [PB[1]], [BkvbA])
        qz = [[sb([128, 256], BF16) for _ in range(2)] for _ in range(2)]
        kT = [sb([128, 256], BF16) for _ in range(2)]
        vg = [sb([128, 512], BF16) for _ in range(2)]
        Qfz = [sb([128, 256], BF16) for _ in range(2)]
        Qbz = [sb([128, 256], BF16) for _ in range(2)]
        for h2 in range(2):
            oth = slice((1 - h2) * 64, (2 - h2) * 64)
            for (t_, B_) in qz[h2] + [Qfz[h2], Qbz[h2]]:
                memset("pool", t_[oth], 0.0, [B_])
        PTm, BPTm = sb([128, 512], BF16)
        oT, BoT = sb([64, 512], F32)
        oc, Boc = sb([128, 4, 64], F32)
        sqj, Bsqj = sb([128, 4, 64], F32)
        sm, Bsm = sb([128, 4], F32)
        vs_, Bvs = sb([128, 4], F32)
        mixt = [sb([128, 256], BF16) for _ in range(2)]
        for ii, n in enumerate(range(nt - 1, -1, -1)):
            k_, Bk = kT[ii % 2]
            v_, Bv = vg[ii % 2]
            rows = slice(n * 128, (n + 1) * 128)
            dma("sp", k_, q.kc[n], [], [Bk])
            dma("sp", v_, q.vcg[rows, :], [], [Bv])
            cp("act", Rb, Rst, [BR], [BRb])
            for h2 in range(2):
                r_ = slice(h2 * 64, (h2 + 1) * 64)
                qh, Bqh = qz[h2][ii % 2]
                dma("sp", qh[r_], q.qc[n][r_], [], [Bqh])
                q3 = qh[r_].rearrange("p (a t) -> p a t", t=128)
                tt("dve", Qfz[h2][0][r_].rearrange("p (a t) -> p a t", t=128), q3, qdf[r_], ALU.mult,
                   [Bqh, Bqdf], [Qfz[h2][1]])
                tt("pool", Qbz[h2][0][r_].rearrange("p (a t) -> p a t", t=128), q3, qdb[r_], ALU.mult,
                   [Bqh, Bqdb], [Qbz[h2][1]])
            for h in range(4):
                pr, h2 = h // 2, h % 2
                qh, Bqh = qz[h2][ii % 2]
                mm(ps(2)[:, h * 128:(h + 1) * 128], k_[:, pr * 128:(pr + 1) * 128], qh[:, pr * 128:(pr + 1) * 128],
                   True, True, [Bk, Bqh], [PB[2]])
            tt("dve", PTm.rearrange("p (h t) -> p h t", t=128), ps(2).rearrange("p (h t) -> p h t", t=128), maskT,
               ALU.mult, [PB[2], Bmask], [BPTm])
            for h in range(4):
                pr, h2 = h // 2, h % 2
                r_ = slice(h2 * 64, (h2 + 1) * 64)
                po = ps(3)[0:64, h * 128:(h + 1) * 128]
                mm(po, v_[:, h * 64:(h + 1) * 64], PTm[:, h * 128:(h + 1) * 128], True, False, [Bv, BPTm], [PB[3]])
                mm(po, Sall[:, n, pr * 64:(pr + 1) * 64], Qfz[h2][0][:, pr * 128:(pr + 1) * 128], False, False,
                   [BSall, Qfz[h2][1]], [PB[3]])
                mm(po, Rb[:, pr * 64:(pr + 1) * 64], Qbz[h2][0][:, pr * 128:(pr + 1) * 128], False, True,
                   [BRb, Qbz[h2][1]], [PB[3]])
            R3 = Rst.rearrange("p (a e) -> p a e", e=64)
            tt("dve", R3, R3, bc(dCb.unsqueeze(2), [128, 2, 64]), ALU.mult, [BR, BdCb, BRb], [BR])
            tt("dve", Rst, Rst, kvbA[:, n, :], ALU.add, [BR, BkvbA], [BR])
            cp("act", oT, ps(3)[0:64, :], [PB[3]], [BoT])
            for h in range(4):
                tr(ps(4)[:, h * 64:(h + 1) * 64], oT[0:64, h * 128:(h + 1) * 128], ident_f[0:64, 0:64],
                   [BoT, Bidf], [PB[4]])
            pe3 = ps(4)[:, 0:256].rearrange("p (h e) -> p h e", e=64)
            rsum(sm, pe3, [PB[4]], [Bsm])
            ts("dve", sm, sm, 1.0 / 64, None, ALU.mult, None, [Bsm], [Bsm])
            tt("dve", oc, pe3, bc(sm.unsqueeze(2), [128, 4, 64]), ALU.subtract, [PB[4], Bsm], [Boc])
            act(sqj, oc, AF.Square, [Boc], [Bsqj])
            rsum(vs_, sqj, [Bsqj], [Bvs])
            act(vs_, vs_, AF.Sqrt, [Bvs, Beps], [Bvs], scale=1.0 / 64, bias=epsb)
            recip(vs_, vs_, [Bvs], [Bvs])
            tt("dve", oc, oc, bc(vs_.unsqueeze(2), [128, 4, 64]), ALU.mult, [Boc, Bvs], [Boc])
            ocf = oc.rearrange("p h e -> p (h e)")
            tt("dve", ocf, ocf, cgn, ALU.mult, [Boc, Bcgn], [Boc])
            m_, Bm = mixt[ii % 2]
            tt("dve", m_, ocf, v_[:, 256:512], ALU.mult, [Boc, Bv], [Bm])
            dma("pool", q.mix[rows, 768:1024], m_, [Bm], [])
        S.barrier()

    def phaseCD(l):
        st["off"] = PERSIST
        wout, Bwout = sb([128, KC, D], BF16)
        wxq, Bwxq = sb([128, KC, D], BF16)
        wxo, Bwxo = sb([128, KC, D], BF16)
        load_w(wout, Bwout, W["w_out"][l], KC)
        load_w(wxq, Bwxq, W["w_xq"][l], KC)
        load_w(wxo, Bwxo, W["w_xo"][l], KC)
        gcr, Bgcr = sb([128, D], F32)
        load_bc(gcr, Bgcr, W["g_cross"][l])
        mKT = [sb([128, KC, MEM], BF16) for _ in seqs]
        mV = [sb([128, 2, D], BF16) for _ in seqs]
        keep = st["off"]
        wxk, Bwxk = sb([128, KC, D], BF16)
        wxv, Bwxv = sb([128, KC, D], BF16)
        load_w(wxk, Bwxk, W["w_xk"][l], KC)
        load_w(wxv, Bwxv, W["w_xv"][l], KC)
        gme, Bgme = sb([128, D], F32)
        load_bc(gme, Bgme, W["g_mem"][l])
        mt_, Bmt = sb([128, D], F32)
        junk = sb([128, D], BF16)
        ss = sb([128, 1], F32)
        rstd = sb([128, 1], F32)
        scr = (junk[0], junk[1], ss[0], ss[1], rstd[0], rstd[1])
        hb, Bhb = sb([128, D], BF16)
        mT, BmT = sb([128, KC, MEM], BF16)
        for q in seqs:
            for c in range(2):
                dma("sp", mt_, q.mem[c * 128:(c + 1) * 128, :], [], [Bmt])
                rmsnorm_to_bf16(mt_, Bmt, gme, Bgme, hb, Bhb, scr)
                transpose8(hb, Bhb, mT[:, :, c * 128:(c + 1) * 128], BmT, 0, "act")
            K_, BK = mKT[q.i]
            V_, BV = mV[q.i]
            for f in range(KC):
                b = 1 + f % 2
                for k in range(KC):
                    mm(ps(b)[:, 0:MEM], wxk[:, k, f * 128:(f + 1) * 128], mT[:, k, :], k == 0, k == KC - 1,
                       [Bwxk, BmT], [PB[b]])
                cp("act", K_[:, f, :], ps(b)[:, 0:MEM], [PB[b]], [BK])
            for c in range(2):
                for hf in range(2):
                    b = 3 + hf
                    for k in range(KC):
                        mm(ps(b), mT[:, k, c * 128:(c + 1) * 128], wxv[:, k, hf * 512:(hf + 1) * 512], k == 0,
                           k == KC - 1, [BmT, Bwxv], [PB[b]])
                    cp("dve", V_[:, c, hf * 512:(hf + 1) * 512], ps(b), [PB[b]], [BV])
        S.barrier()
        st["off"] = keep
        G = 4
        xt = [sb([128, D], F32) for _ in range(G)]
        mx = [sb([128, D], BF16) for _ in range(2)]
        junk = sb([128, D], BF16)
        ss = sb([128, 1], F32)
        rstd = sb([128, 1], F32)
        scr = (junk[0], junk[1], ss[0], ss[1], rstd[0], rstd[1])
        hb, Bhb = sb([128, D], BF16)
        mixT, BmixT = sb([128, KC, 128], BF16)
        hT, BhT = sb([128, KC, 512], BF16)
        QT, BQT = sb([128, KC, 512], BF16)
        PT, BPT = sb([128, 8, 512], BF16)
        OT, BOT = sb([128, KC, 512], BF16)
        rden, Brden = sb([128, 512], F32)
        for q in seqs:
            src = q.x_in if l == 0 else q.xs
            K_, BK = mKT[q.i]
            V_, BV = mV[q.i]
            for gi in range(q.nt // G):
                for j in range(G):
                    t = gi * G + j
                    rows = slice(t * 128, (t + 1) * 128)
                    x_, Bx = xt[j]
                    m_, Bm = mx[j % 2]
                    dma("sp", x_, src[rows, :], [], [Bx])
                    dma("sp", m_, q.mix[rows, :], [], [Bm])
                    transpose8(m_, Bm, mixT, BmixT, 0, "act")
                    for hf in range(2):
                        for k in range(KC):
                            mm(ps(1 + hf), mixT[:, k, :], wout[:, k, hf * 512:(hf + 1) * 512], k == 0, k == KC - 1,
                               [BmixT, Bwout], [PB[1 + hf]])
                    tt("dve", x_, x_, ps(1, 2), ALU.add, [Bx, PB[1], PB[2]], [Bx])
                    rmsnorm_to_bf16(x_, Bx, gcr, Bgcr, hb, Bhb, scr)
                    transpose8(hb, Bhb, hT[:, :, j * 128:(j + 1) * 128], BhT, 0, "act")
                for f in range(KC):
                    b = 3 + f % 2
                    for k in range(KC):
                        mm(ps(b), wxq[:, k, f * 128:(f + 1) * 128], hT[:, k, :], k == 0, k == KC - 1,
                           [Bwxq, BhT], [PB[b]])
                    cp("act" if f % 2 else "dve", QT[:, f, :], ps(b), [PB[b]], [BQT])
                for h in range(4):
                    for mc in range(2):
                        b = 5 + mc
                        for dc in range(2):
                            mm(ps(b), K_[:, 2 * h + dc, mc * 128:(mc + 1) * 128], QT[:, 2 * h + dc, :], dc == 0,
                               dc == 1, [BK, BQT], [PB[b]])
                        act(PT[:, 2 * h + mc, :], ps(b), AF.Exp, [PB[b]], [BPT], scale=1.0 / 16)
                for h in range(4):
                    for mc in range(2):
                        mm(ps(7), ones_b, PT[:, 2 * h + mc, :], mc == 0, mc == 1, [Bones, BPT], [PB[7]])
                    recip(rden, ps(7), [PB[7]], [Brden])
                    for dc in range(2):
                        b = 3 + dc
                        f = 2 * h + dc
                        for mc in range(2):
                            mm(ps(b), V_[:, mc, f * 128:(f + 1) * 128], PT[:, 2 * h + mc, :], mc == 0, mc == 1,
                               [BV, BPT], [PB[b]])
                        tt("dve", OT[:, f, :], ps(b), rden, ALU.mult, [PB[b], Brden], [BOT])
                for j in range(G):
                    t = gi * G + j
                    x_, Bx = xt[j]
                    for hf in range(2):
                        for f in range(KC):
                            mm(ps(1 + hf), OT[:, f, j * 128:(j + 1) * 128], wxo[:, f, hf * 512:(hf + 1) * 512], f == 0,
                               f == KC - 1, [BOT, Bwxo], [PB[1 + hf]])
                    tt("dve", x_, x_, ps(1, 2), ALU.add, [Bx, PB[1], PB[2]], [Bx])
                    dma("pool", q.xs[t * 128:(t + 1) * 128, :], x_, [Bx], [])
        S.barrier()

    def phaseE(l):
        st["off"] = PERSIST
        last = l == DEPTH - 1
        wg, Bwg = sb([128, KC, DFF], BF16)
        wu, Bwu = sb([128, KC, DFF], BF16)
        wd, Bwd = sb([128, FC, D], BF16)
        load_w(wg, Bwg, W["w_gate"][l], KC)
        load_w(wu, Bwu, W["w_up"][l], KC)
        load_w(wd, Bwd, W["w_down"][l], FC)
        gff, Bgff = sb([128, D], F32)
        load_bc(gff, Bgff, W["g_ffn"][l])
        if last:
            gfi, Bgfi = sb([128, D], F32)
            load_bc(gfi, Bgfi, W["g_final"])
        G = 4
        xt = [sb([128, D], F32) for _ in range(2)]
        junk = sb([128, D], BF16)
        ss = sb([128, 1], F32)
        rstd = sb([128, 1], F32)
        scr = (junk[0], junk[1], ss[0], ss[1], rstd[0], rstd[1])
        hb, Bhb = sb([128, D], BF16)
        hT, BhT = sb([128, KC, 512], BF16)
        sg = [sb([128, 512], F32) for _ in range(2)]
        aT, BaT = sb([128, FC, 512], BF16)
        for q in seqs:
            for gi in range(q.nt // G):
                for j in range(G):
                    t = gi * G + j
                    x_, Bx = xt[j % 2]
                    dma("sp", x_, q.xs[t * 128:(t + 1) * 128, :], [], [Bx])
                    rmsnorm_to_bf16(x_, Bx, gff, Bgff, hb, Bhb, scr)
                    transpose8(hb, Bhb, hT[:, :, j * 128:(j + 1) * 128], BhT, 0, "act")
                for f in range(FC):
                    bg, bu = 1 + f % 2, 3 + f % 2
                    for k in range(KC):
                        mm(ps(bg), wg[:, k, f * 128:(f + 1) * 128], hT[:, k, :], k == 0, k == KC - 1,
                           [Bwg, BhT], [PB[bg]])
                    for k in range(KC):
                        mm(ps(bu), wu[:, k, f * 128:(f + 1) * 128], hT[:, k, :], k == 0, k == KC - 1,
                           [Bwu, BhT], [PB[bu]])
                    s_, Bs = sg[f % 2]
                    act(s_, ps(bg), AF.Silu, [PB[bg]], [Bs])
                    tt("dve", aT[:, f, :], s_, ps(bu), ALU.mult, [Bs, PB[bu]], [BaT])
                for j in range(G):
                    t = gi * G + j
                    x_, Bx = xt[j % 2]
                    dma("sp", x_, q.xs[t * 128:(t + 1) * 128, :], [], [Bx])
                    for hf in range(2):
                        for f in range(FC):
                            mm(ps(5 + hf), aT[:, f, j * 128:(j + 1) * 128], wd[:, f, hf * 512:(hf + 1) * 512], f == 0,
                               f == FC - 1, [BaT, Bwd], [PB[5 + hf]])
                    tt("dve", x_, x_, ps(5, 2), ALU.add, [Bx, PB[5], PB[6]], [Bx])
                    if last:
                        act(scr[0], x_, AF.Square, [Bx], [scr[1], scr[3]], accum_out=scr[2])
                        act(scr[2], scr[2], AF.Sqrt, [scr[3], Beps], [scr[3]], scale=1.0 / D, bias=epsb)
                        recip(scr[4], scr[2], [scr[3]], [scr[5]])
                        stt("dve", x_, x_, scr[4][:, 0:1], gfi, ALU.mult, ALU.mult, [Bx, scr[5], Bgfi], [Bx])
                        dma("pool", q.y[t * 128:(t + 1) * 128, :], x_, [Bx], [])
                    else:
                        dma("pool", q.xs[t * 128:(t + 1) * 128, :], x_, [Bx], [])
        S.barrier()

    setup()
    for l in range(DEPTH):
        phaseA(l)
        for q in seqs:
            attn(l, q, "a")
            attn(l, q, "b")
            retention(l, q)
        phaseCD(l)
        phaseE(l)
    S.emit()
    es.close()
    return nc


WNAMES = ("g_mix", "w_in", "a_q_norm", "a_k_norm", "a_out_norm", "b_sink", "b_out_norm", "c_decay_fwd",
          "c_decay_bwd", "c_gn", "w_out", "g_cross", "g_mem", "w_xq", "w_xk", "w_xv", "w_xo", "g_ffn", "w_gate",
          "w_up", "w_down", "g_final")


def make_in_maps(inputs, n_cores=8):
    xp, xsm = inputs["x_prompt"], inputs["x_sample"]
    mp, ms = inputs["mem_prompt"], inputs["mem_sample"]
    wts = {k: np.ascontiguousarray(np.asarray(inputs[k], dtype=np.float32)) for k in WNAMES}
    maps = []
    for c in range(n_cores):
        m = dict(wts)
        m["x0"] = np.ascontiguousarray(xp[c % xp.shape[0]], dtype=np.float32)
        m["m0"] = np.ascontiguousarray(mp[c % mp.shape[0]], dtype=np.float32)
        m["x1"] = np.ascontiguousarray(xsm[c % xsm.shape[0]], dtype=np.float32)
        m["m1"] = np.ascontiguousarray(ms[c % ms.shape[0]], dtype=np.float32)
        maps.append(m)
    return maps


def kernel(**inputs):
    xp, xsm = np.asarray(inputs["x_prompt"]), np.asarray(inputs["x_sample"])
    depth = np.asarray(inputs["w_in"]).shape[0]
    nc = build(xp.shape[1], xsm.shape[1], depth)
    maps = make_in_maps({k: np.asarray(v) for k, v in inputs.items()})
    res = run_bass_kernel_spmd(nc, maps, core_ids=list(range(8)))
    yp = np.stack([res.results[b]["y0"] for b in range(xp.shape[0])], axis=0).astype(np.float32)
    ysm = np.stack([res.results[b]["y1"] for b in range(xsm.shape[0])], axis=0).astype(np.float32)
    return (yp, ysm)
```

```python
import math
from contextlib import ExitStack

import numpy as np
import concourse.bass as bass
import concourse.mybir as mybir
from concourse.bass_utils import run_bass_kernel_spmd

F32 = mybir.dt.float32
BF16 = mybir.dt.bfloat16
I32 = mybir.dt.int32
AF = mybir.ActivationFunctionType
ALU = mybir.AluOpType
AX = mybir.AxisListType

D = 1024
KC = 8
DFF = 2816
FC = 22
MEM = 256
EPS = 1e-6
THETA = 10000.0
KRING = 8
ATTACH = True


class Buf:
    __slots__ = ("name", "w", "rs", "rd")

    def __init__(self, name):
        self.name = name
        self.w = None
        self.rs = {}
        self.rd = []


class Op:
    __slots__ = ("eng", "fn", "deps", "need", "val", "dma", "slot", "dval")


class Sched:
    ENG = ("pe", "act", "dve", "pool", "sp")

    def __init__(self, nc):
        self.nc = nc
        self.h = {"pe": nc.tensor, "act": nc.scalar, "dve": nc.vector, "pool": nc.gpsimd, "sp": nc.sync}
        self.ops = {e: [] for e in self.ENG}
        self.dmas = {e: [] for e in self.ENG}
        self.pending = []

    def op(self, eng, fn, r=(), w=(), dma=False):
        o = Op()
        o.eng, o.fn, o.dma, o.need, o.val = eng, fn, dma, False, 0
        deps = []
        for b in r:
            if b.w is not None:
                deps.append(b.w)
        for b in w:
            if b.w is not None and not b.rs and not b.rd:
                deps.append(b.w)
            deps.extend(b.rs.values())
            deps.extend(b.rd)
        if dma:
            n = len(self.dmas[eng])
            o.slot, o.dval = n % KRING, 16 * (n // KRING + 1)
            if n >= KRING:
                deps.append(self.dmas[eng][n - KRING])
            self.dmas[eng].append(o)
            self.pending.append(o)
        o.deps = [d for d in deps if d.dma or not (eng == "pe" and d.eng == "pe")]
        for d in o.deps:
            if not d.dma:
                d.need = True
        for b in r:
            if dma:
                b.rd.append(o)
            else:
                b.rs[eng] = o
        for b in w:
            b.w, b.rs, b.rd = o, {}, []
        self.ops[eng].append(o)
        return o

    def barrier(self):
        deps = list(self.pending)
        for e in self.ENG:
            for o in reversed(self.ops[e]):
                if o.fn is not None and not o.dma:
                    deps.append(o)
                    o.need = True
                    break
        self.pending = []
        for e in self.ENG:
            o = Op()
            o.eng, o.fn, o.dma, o.need, o.val, o.deps = e, None, False, False, 0, deps
            self.ops[e].append(o)

    def emit(self):
        nc = self.nc
        with ExitStack() as es:
            csem = {e: es.enter_context(nc.semaphore("c_" + e)) for e in self.ENG}
            dsem = {e: [es.enter_context(nc.semaphore("d_%s%d" % (e, i))) for i in range(KRING)]
                    for e in self.ENG if self.dmas[e]}
            for e in self.ENG:
                c = 0
                for o in self.ops[e]:
                    if o.need and not o.dma and o.fn is not None:
                        c += 1
                        o.val = c
            block = es.enter_context(nc.Block())
            names = {"pe": "tensor", "act": "scalar", "dve": "vector", "pool": "gpsimd", "sp": "sync"}

            def make(e):
                def body(h):
                    known = {}
                    for o in self.ops[e]:
                        waits = {}
                        for d in o.deps:
                            key, val = ((d.eng, d.slot), d.dval) if d.dma else ((d.eng, -1), d.val)
                            if known.get(key, 0) >= val:
                                continue
                            if waits.get(key, 0) < val:
                                waits[key] = val
                        wl = list(waits.items())
                        attach = None
                        if ATTACH and wl and o.fn is not None and not o.dma and e in ("act", "dve", "pool"):
                            attach = wl.pop()
                        for key, val in wl:
                            sem = dsem[key[0]][key[1]] if key[1] >= 0 else csem[key[0]]
                            h.wait_ge(sem, val)
                            known[key] = val
                        if o.fn is None:
                            continue
                        ins = o.fn(h)
                        if attach is not None:
                            key, val = attach
                            sem = dsem[key[0]][key[1]] if key[1] >= 0 else csem[key[0]]
                            ins._wait_ge(sem, val)
                            known[key] = val
                        if o.dma:
                            ins.then_inc(dsem[e][o.slot], 16)
                        elif o.need:
                            ins.then_inc(csem[e], 1)
                return body

            for e in self.ENG:
                getattr(block, names[e])(make(e))


class Seq:
    pass


def build(T1, T2, DEPTH):
    nc = bass.Bass("TRN2", target_bir_lowering=False)
    S = Sched(nc)
    es = ExitStack()

    def din(name, shape):
        return nc.dram_tensor(name, list(shape), F32, kind="ExternalInput").ap()

    def dscr(name, shape, dt):
        return nc.dram_tensor(name, list(shape), dt, kind="Internal").ap()

    L = DEPTH
    W = dict(
        g_mix=din("g_mix", (L, D)), w_in=din("w_in", (L, D, 2304)), a_q_norm=din("a_q_norm", (L, 64)),
        a_k_norm=din("a_k_norm", (L, 64)), a_out_norm=din("a_out_norm", (L, 512)), b_sink=din("b_sink", (L, 4)),
        b_out_norm=din("b_out_norm", (L, 256)), c_decay_fwd=din("c_decay_fwd", (L, 4)),
        c_decay_bwd=din("c_decay_bwd", (L, 4)), c_gn=din("c_gn", (L, 256)), w_out=din("w_out", (L, D, D)),
        g_cross=din("g_cross", (L, D)), g_mem=din("g_mem", (L, D)), w_xq=din("w_xq", (L, D, D)),
        w_xk=din("w_xk", (L, D, D)), w_xv=din("w_xv", (L, D, D)), w_xo=din("w_xo", (L, D, D)),
        g_ffn=din("g_ffn", (L, D)), w_gate=din("w_gate", (L, D, DFF)), w_up=din("w_up", (L, D, DFF)),
        w_down=din("w_down", (L, DFF, D)), g_final=din("g_final", (D,)),
    )
    seqs = []
    for si, T in enumerate((T1, T2)):
        q = Seq()
        q.T, q.nt, q.i = T, T // 128, si
        q.x_in = din("x%d" % si, (T, D))
        q.mem = din("m%d" % si, (MEM, D))
        q.y = nc.dram_tensor("y%d" % si, [T, D], F32, kind="ExternalOutput").ap()
        n = q.nt
        q.xs = dscr("xs%d" % si, (T, D), F32)
        q.qa = dscr("qa%d" % si, (n, 128, 512), BF16)
        q.ka = dscr("ka%d" % si, (n, 128, 128), BF16)
        q.va = dscr("va%d" % si, (T, 130), BF16)
        q.qb = dscr("qb%d" % si, (n, 128, 256), BF16)
        q.kb = dscr("kb%d" % si, (n, 128, 128), BF16)
        q.vb = dscr("vb%d" % si, (T, 130), BF16)
        q.qc = dscr("qc%d" % si, (n, 128, 256), BF16)
        q.kc = dscr("kc%d" % si, (n, 128, 256), BF16)
        q.kct = dscr("kct%d" % si, (T, 256), BF16)
        q.vcg = dscr("vcg%d" % si, (T, 512), BF16)
        q.mix = dscr("mix%d" % si, (T, D), BF16)
        seqs.append(q)
    NTM = max(q.nt for q in seqs)
    tab_s = dscr("tab_s", (NTM, 128, 256), F32)

    ARENA = 98000
    arena = es.enter_context(nc.sbuf_tensor("arena", [128, ARENA], BF16))
    psum = es.enter_context(nc.psum_tensor("psum", [128, 4096], F32))
    PB = [Buf("psb%d" % i) for i in range(8)]
    st = {"off": 0, "n": 0}

    def sb(shape, dt, name=None):
        n = int(np.prod(shape[1:]))
        nb = n * (2 if dt in (F32, I32) else 1)
        nb_al = (nb + 15) // 16 * 16
        off = st["off"]
        assert off + nb_al <= ARENA, "SBUF arena overflow %d" % (off + nb_al)
        st["off"] = off + nb_al
        v = arena[0:shape[0], off:off + nb]
        if dt != BF16:
            v = v.bitcast(dt)
        if len(shape) == 3:
            v = v.rearrange("p (a b) -> p a b", b=shape[2])
        elif len(shape) == 4:
            v = v.rearrange("p (a b c) -> p a b c", b=shape[2], c=shape[3])
        st["n"] += 1
        return v, Buf(name or "t%d" % st["n"])

    def ps(bank, nb=1, dt=F32):
        v = psum[:, bank * 512:(bank + nb) * 512]
        return v.bitcast(dt) if dt != F32 else v

    def mm(out, lhsT, rhs, start, stop, r, w):
        return S.op("pe", lambda e: e.matmul(out, lhsT=lhsT, rhs=rhs, start=start, stop=stop), r, w)

    def tr(out, in_, ident, r, w):
        return S.op("pe", lambda e: e.transpose(out=out, in_=in_, identity=ident), r, w)

    def act(out, in_, func, r, w, **kw):
        return S.op("act", lambda e: e.activation(out=out, in_=in_, func=func, **kw), r, w)

    def tt(eng, out, in0, in1, op, r, w):
        return S.op(eng, lambda e: e.tensor_tensor(out=out, in0=in0, in1=in1, op=op), r, w)

    def ts(eng, out, in0, s1, s2, op0, op1, r, w):
        if op1 is None:
            return S.op(eng, lambda e: e.tensor_scalar(out=out, in0=in0, scalar1=s1, scalar2=None, op0=op0), r, w)
        return S.op(eng, lambda e: e.tensor_scalar(out=out, in0=in0, scalar1=s1, scalar2=s2, op0=op0, op1=op1), r, w)

    def stt(eng, out, in0, scalar, in1, op0, op1, r, w):
        return S.op(eng, lambda e: e.scalar_tensor_tensor(out=out, in0=in0, scalar=scalar, in1=in1, op0=op0, op1=op1), r, w)

    def cp(eng, out, in_, r, w):
        if eng == "act":
            return act(out, in_, AF.Copy, r, w)
        return S.op(eng, lambda e: e.tensor_copy(out=out, in_=in_), r, w)

    def recip(out, in_, r, w):
        return S.op("dve", lambda e: e.reciprocal(out=out, in_=in_), r, w)

    def rsum(out, in_, r, w):
        return S.op("dve", lambda e: e.reduce_sum(out=out, in_=in_, axis=AX.X), r, w)

    def memset(eng, out, val, w):
        return S.op(eng, lambda e: e.memset(out, val), (), w)

    def iota(out, pattern, base, cm, w):
        return S.op("pool", lambda e: e.iota(out, pattern=pattern, base=base, channel_multiplier=cm,
                                             allow_small_or_imprecise_dtypes=True), (), w)

    def dma(q, out, in_, r, w):
        return S.op(q, lambda e: e.dma_start(out=out, in_=in_), r, w, dma=True)

    def bc(ap, shape):
        return ap.to_broadcast(list(shape))

    ident_b, Bidb = sb([128, 128], BF16, "ident_b")
    ident_f, Bidf = sb([128, 128], F32, "ident_f")
    ones_b, Bones = sb([128, 128], BF16, "ones_b")
    epsb, Beps = sb([128, 1], F32, "eps")
    triL, BtriL = sb([128, 128], BF16, "triL")
    triR, BtriR = sb([128, 128], BF16, "triR")
    posD, BposD = sb([128, 128], F32, "posD")
    negD, BnegD = sb([128, 128], F32, "negD")
    aidx1, Ba1 = sb([128, 128], F32, "aidx1")
    aidxC, BaC = sb([128, 128], F32, "aidxC")
    cidx, Bcidx = sb([128, 1], F32, "cidx")
    c127, Bc127 = sb([128, 1], F32, "c127")
    PERSIST = st["off"]

    def setup():
        Dm, BD = sb([128, 128], F32, "Dm")
        iota(Dm, [[1, 128]], 0, -1, [BD])
        ts("dve", ident_f, Dm, 0.0, None, ALU.is_equal, None, [BD], [Bidf])
        ts("dve", ident_b, Dm, 0.0, None, ALU.is_equal, None, [BD], [Bidb])
        ts("dve", triL, Dm, 0.0, None, ALU.is_le, None, [BD], [BtriL])
        ts("dve", triR, Dm, 0.0, None, ALU.is_ge, None, [BD], [BtriR])
        ts("dve", posD, Dm, 0.0, None, ALU.max, None, [BD], [BposD])
        ts("dve", negD, Dm, -1.0, 0.0, ALU.mult, ALU.max, [BD], [BnegD])
        memset("pool", ones_b, 1.0, [Bones])
        memset("pool", epsb, EPS, [Beps])
        iota(aidx1, [[1, 128]], 1, 0, [Ba1])
        iota(aidxC, [[-1, 128]], 128, 0, [BaC])
        iota(cidx, [[1, 1]], 0, 1, [Bcidx])
        iota(c127, [[1, 1]], 127, -1, [Bc127])
        NT = NTM
        pos, Bpos = sb([128, NT], F32)
        iota(pos, [[128, NT]], 0, 1, [Bpos])
        hi, Bhi = sb([128, 1], F32)
        ts("dve", hi, cidx, 64.0, None, ALU.is_ge, None, [Bcidx], [Bhi])
        colv, Bcol = sb([128, 1], F32)
        stt("dve", colv, hi, -64.0, cidx, ALU.mult, ALU.add, [Bhi, Bcidx], [Bcol])
        rowv, Brow = sb([128, NT], F32)
        iota(rowv, [[2, NT]], 0, 0, [Brow])
        ts("dve", rowv, rowv, hi[:, 0:1], None, ALU.add, None, [Brow, Bhi], [Brow])
        i32t, Bi32 = sb([128, 32], F32)
        iota(i32t, [[1, 32]], 0, 0, [Bi32])
        inv32, Binv32 = sb([128, 32], F32)
        act(inv32, i32t, AF.Exp, [Bi32], [Binv32], scale=-math.log(THETA) / 32.0)
        inv16, Binv16 = sb([128, 16], F32)
        act(inv16, i32t[:, 0:16], AF.Exp, [Bi32], [Binv16], scale=-math.log(THETA) / 16.0)
        angs, Bangs = sb([128, NT, 32], F32)
        tt("dve", angs, bc(pos.unsqueeze(2), [128, NT, 32]), bc(inv32.unsqueeze(1), [128, NT, 32]), ALU.mult,
           [Bpos, Binv32], [Bangs])
        anga, Banga = sb([128, NT, 32], F32)
        tt("dve", anga[:, :, 0:16], bc(rowv.unsqueeze(2), [128, NT, 16]), bc(inv16.unsqueeze(1), [128, NT, 16]),
           ALU.mult, [Brow, Binv16], [Banga])
        ts("dve", anga[:, :, 16:32], bc(inv16.unsqueeze(1), [128, NT, 16]), colv[:, 0:1], None, ALU.mult, None,
           [Binv16, Bcol, Banga], [Banga])
        tab, Btab = sb([128, NT, 256], F32)
        u, Bu = sb([128, NT, 32], F32)
        ni, Bni = sb([128, NT, 32], I32)
        nf, Bnf = sb([128, NT, 32], F32)
        sn, Bsn = sb([128, NT, 32], F32)

        def sinshift(ang, Bang, shift):
            ts("dve", u, ang, 1.0 / (2 * math.pi), 0.5 + shift / (2 * math.pi), ALU.mult, ALU.add, [Bang], [Bu])
            cp("dve", ni, u, [Bu], [Bni])
            cp("dve", nf, ni, [Bni], [Bnf])
            tt("dve", u, u, nf, ALU.subtract, [Bu, Bnf], [Bu])
            ts("dve", nf, u, 0.0, None, ALU.is_lt, None, [Bu], [Bnf])
            tt("dve", u, u, nf, ALU.add, [Bu, Bnf], [Bu])
            ts("dve", nf, u, 1.0, None, ALU.is_ge, None, [Bu], [Bnf])
            tt("dve", u, u, nf, ALU.subtract, [Bu, Bnf], [Bu])
            ts("dve", u, u, 2 * math.pi, -math.pi, ALU.mult, ALU.add, [Bu], [Bu])
            ts("dve", u, u, -3.14159, 3.14159, ALU.max, ALU.min, [Bu], [Bu])
            act(sn, u, AF.Sin, [Bu], [Bsn])

        for base, ang, Bang in ((0, anga, Banga), (128, angs, Bangs)):
            sinshift(ang, Bang, math.pi / 2)
            cp("dve", tab[:, :, base:base + 32], sn, [Bsn], [Btab])
            cp("dve", tab[:, :, base + 32:base + 64], sn, [Bsn], [Btab])
            sinshift(ang, Bang, 0.0)
            ts("dve", tab[:, :, base + 64:base + 96], sn, -1.0, None, ALU.mult, None, [Bsn], [Btab])
            cp("dve", tab[:, :, base + 96:base + 128], sn, [Bsn], [Btab])
        dma("sp", tab_s.rearrange("t p c -> p t c"), tab, [Btab], [])
        S.barrier()

    def rmsnorm_to_bf16(xt, Bx, gain, Bg, hb, Bhb, scr):
        junk, Bj, ss, Bss, rstd, Brs = scr
        act(junk, xt, AF.Square, [Bx], [Bj, Bss], accum_out=ss)
        act(ss, ss, AF.Sqrt, [Bss, Beps], [Bss], scale=1.0 / D, bias=epsb)
        recip(rstd, ss, [Bss], [Brs])
        stt("dve", hb, xt, rstd[:, 0:1], gain, ALU.mult, ALU.mult, [Bx, Brs, Bg], [Bhb])

    def transpose8(src, Bsrc, dst, Bdst, bank, cpeng):
        pt = ps(bank, 1, BF16)
        for k in range(KC):
            tr(pt[:, k * 128:(k + 1) * 128], src[:, k * 128:(k + 1) * 128], ident_b, [Bsrc, Bidb], [PB[bank]])
        cp(cpeng, dst, pt.rearrange("p (k t) -> p k t", t=128), [PB[bank]], [Bdst])

    def load_w(dst, Bdst, src2d, nk):
        dma("pool", dst, src2d.rearrange("(k p) n -> p k n", p=128), [], [Bdst])

    def load_bc(dst, Bdst, src1d):
        dma("sp", dst, src1d.partition_broadcast(128), [], [Bdst])

    def phaseA(l):
        st["off"] = PERSIST
        win, Bwin = sb([128, KC, 2304], BF16, "win")
        wi = W["w_in"][l]
        for k in range(KC):
            rows = wi[k * 128:(k + 1) * 128, :]
            for j in range(4):
                dma("pool", win[:, k, j * 128:(j + 1) * 128].rearrange("p (g d) -> p g d", g=2),
                    rows[:, 0:512].rearrange("p (g j d) -> p j g d", g=2, j=4)[:, j, :, :], [], [Bwin])
            for j in range(2):
                dma("pool", win[:, k, 640 + j * 128:640 + (j + 1) * 128].rearrange("p (g d) -> p g d", g=2),
                    rows[:, 768:1024].rearrange("p (g j d) -> p j g d", g=2, j=2)[:, j, :, :], [], [Bwin])
        for (d0, d1, s0) in ((512, 640, 512), (896, 1024, 1024), (1024, 1536, 1280), (1536, 1664, 640),
                             (1664, 1792, 1152), (1792, 2304, 1792)):
            dma("pool", win[:, :, d0:d1], wi[:, s0:s0 + (d1 - d0)].rearrange("(k p) n -> p k n", p=128), [], [Bwin])
        gmix, Bgmix = sb([128, D], F32)
        load_bc(gmix, Bgmix, W["g_mix"][l])
        gqk, Bgqk = sb([128, 10, 64], F32)
        dma("sp", gqk[:, 0:8, :], bc(W["a_q_norm"][l].partition_broadcast(128).unsqueeze(1), [128, 8, 64]), [], [Bgqk])
        dma("sp", gqk[:, 8:10, :], bc(W["a_k_norm"][l].partition_broadcast(128).unsqueeze(1), [128, 2, 64]), [], [Bgqk])
        xt = [sb([128, D], F32) for _ in range(2)]
        tabt = [sb([128, 256], F32) for _ in range(2)]
        junk = sb([128, D], BF16)
        ss = sb([128, 1], F32)
        rstd = sb([128, 1], F32)
        scr = (junk[0], junk[1], ss[0], ss[1], rstd[0], rstd[1])
        hb, Bhb = sb([128, D], BF16)
        hT, BhT = sb([128, KC, 128], BF16)
        pfs = [sb([128, 2304], F32) for _ in range(2)]
        sq, Bsq = sb([128, 640], F32)
        ssh, Bssh = sb([128, 10], F32)
        rsh, Brsh = sb([128, 10], F32)
        ra, Bra = sb([128, 896], F32)
        rb, Brb = sb([128, 896], F32)
        pb = [sb([128, 1536], BF16) for _ in range(2)]
        vaa = [sb([128, 2, 65], BF16) for _ in range(2)]
        vab = [sb([128, 2, 65], BF16) for _ in range(2)]
        vcg = [sb([128, 512], BF16) for _ in range(2)]
        pT = [sb([128, 12, 128], BF16) for _ in range(2)]
        for v_, B_ in vaa + vab:
            memset("pool", v_, 1.0, [B_])
        tiles = [(q, t) for q in seqs for t in range(q.nt)]

        def front(i):
                q, t = tiles[i]
                s = i % 2
                src = q.x_in if l == 0 else q.xs
                pf, Bpf = pfs[s]
                x_, Bx = xt[s]
                tb, Btb = tabt[s]
                dma("sp", x_, src[t * 128:(t + 1) * 128, :], [], [Bx])
                dma("sp", tb, tab_s[t], [], [Btb])
                rmsnorm_to_bf16(x_, Bx, gmix, Bgmix, hb, Bhb, scr)
                transpose8(hb, Bhb, hT, BhT, 0, "act")
                for cg in range(5):
                    c0 = cg * 512
                    cw = 512 if cg < 4 else 256
                    for k in range(KC):
                        mm(ps(1 + cg)[:, 0:cw], hT[:, k, :], win[:, k, c0:c0 + cw], k == 0, k == KC - 1,
                           [BhT, Bwin], [PB[1 + cg]])

        def evac(i):
                pf, Bpf = pfs[i % 2]
                for cg in range(5):
                    c0 = cg * 512
                    cw = 512 if cg < 4 else 256
                    cp("act", pf[:, c0:c0 + cw], ps(1 + cg)[:, 0:cw], [PB[1 + cg]], [Bpf])

        def back(i):
                q, t = tiles[i]
                s = i % 2
                pf, Bpf = pfs[s]
                tb, Btb = tabt[s]
                act(sq, pf[:, 0:640], AF.Square, [Bpf], [Bsq])
                rsum(ssh, sq.rearrange("p (h d) -> p h d", d=64), [Bsq], [Bssh])
                act(ssh, ssh, AF.Sqrt, [Bssh, Beps], [Bssh], scale=1.0 / 64, bias=epsb)
                recip(rsh, ssh, [Bssh], [Brsh])
                pf3 = pf[:, 0:640].rearrange("p (h d) -> p h d", d=64)
                tt("dve", pf3, pf3, bc(rsh.unsqueeze(2), [128, 10, 64]), ALU.mult, [Bpf, Brsh], [Bpf])
                tt("dve", pf3, pf3, gqk, ALU.mult, [Bpf, Bgqk], [Bpf])
                p_, Bp = pb[s]
                for (c0, nh, tb0) in ((0, 10, 0), (640, 14, 128)):
                    xv = pf[:, c0:c0 + nh * 64].rearrange("p (h d) -> p h d", d=64)
                    av = ra[:, 0:nh * 64].rearrange("p (h d) -> p h d", d=64)
                    bv = rb[:, 0:nh * 64].rearrange("p (h d) -> p h d", d=64)
                    CC = bc(tb[:, tb0:tb0 + 64].unsqueeze(1), [128, nh, 64])
                    nS = bc(tb[:, tb0 + 64:tb0 + 96].unsqueeze(1), [128, nh, 32])
                    pS = bc(tb[:, tb0 + 96:tb0 + 128].unsqueeze(1), [128, nh, 32])
                    tt("dve", av, xv, CC, ALU.mult, [Bpf, Btb], [Bra])
                    tt("pool", bv[:, :, 0:32], xv[:, :, 32:64], nS, ALU.mult, [Bpf, Btb], [Brb])
                    tt("pool", bv[:, :, 32:64], xv[:, :, 0:32], pS, ALU.mult, [Bpf, Btb, Brb], [Brb])
                    tt("dve", p_[:, c0:c0 + nh * 64].rearrange("p (h d) -> p h d", d=64), av, bv, ALU.add,
                       [Bra, Brb], [Bp])
                va_, Bva = vaa[s]
                vb_, Bvb = vab[s]
                vc_, Bvc = vcg[s]
                cp("act", va_[:, :, 0:64], pf[:, 1536:1664].rearrange("p (g d) -> p g d", d=64), [Bpf], [Bva])
                cp("act", vb_[:, :, 0:64], pf[:, 1664:1792].rearrange("p (g d) -> p g d", d=64), [Bpf], [Bvb])
                cp("act", vc_[:, 0:256], pf[:, 1792:2048], [Bpf], [Bvc])
                act(vc_[:, 256:512], pf[:, 2048:2304], AF.Silu, [Bpf], [Bvc])
                pt2 = ps(6, 2, BF16)
                T_, BT = pT[s]
                for j in range(12):
                    bank = 6 if j < 8 else 7
                    tr(pt2[:, j * 128:(j + 1) * 128], p_[:, j * 128:(j + 1) * 128], ident_b, [Bp, Bidb], [PB[bank]])
                cp("dve", T_[:, 0:8, :], pt2[:, 0:1024].rearrange("p (k t) -> p k t", t=128), [PB[6]], [BT])
                cp("dve", T_[:, 8:12, :], pt2[:, 1024:1536].rearrange("p (k t) -> p k t", t=128), [PB[7]], [BT])
                dma("pool", q.qa[t].rearrange("p (j t) -> p j t", t=128), T_[:, 0:4, :], [BT], [])
                dma("pool", q.ka[t], T_[:, 4, :], [BT], [])
                dma("pool", q.qb[t].rearrange("p (j t) -> p j t", t=128), T_[:, 5:7, :], [BT], [])
                dma("pool", q.kb[t], T_[:, 7, :], [BT], [])
                dma("pool", q.qc[t].rearrange("p (j t) -> p j t", t=128), T_[:, 8:10, :], [BT], [])
                dma("pool", q.kc[t].rearrange("p (j t) -> p j t", t=128), T_[:, 10:12, :], [BT], [])
                rows = slice(t * 128, (t + 1) * 128)
                dma("pool", q.kct[rows, :], p_[:, 1280:1536], [Bp], [])
                dma("pool", q.va[rows, :], va_.rearrange("p g d -> p (g d)"), [Bva], [])
                dma("pool", q.vb[rows, :], vb_.rearrange("p g d -> p (g d)"), [Bvb], [])
                dma("pool", q.vcg[rows, :], vc_, [Bvc], [])

        front(0)
        evac(0)
        for i in range(len(tiles)):
            if i + 1 < len(tiles):
                front(i + 1)
            back(i)
            if i + 1 < len(tiles):
                evac(i + 1)
        S.barrier()

    def attn(l, q, kind):
        st["off"] = PERSIST
        nt = q.nt
        nj = 4 if kind == "a" else 2
        qw = nj * 128
        H = 2 * nj
        qs, ks, vs = (q.qa, q.ka, q.va) if kind == "a" else (q.qb, q.kb, q.vb)
        col0 = 0 if kind == "a" else 512
        KT0, BKT0 = sb([128, nt, 128], BF16)
        KT1, BKT1 = sb([128, nt, 128], BF16)
        VA, BVA = sb([128, nt, 130], BF16)
        memset("pool", KT0[64:128], 0.0, [BKT0])
        memset("pool", KT1[0:64], 0.0, [BKT1])
        ksr = ks.rearrange("t p c -> p t c")
        dma("sp", KT0[0:64], ksr[0:64], [], [BKT0])
        dma("sp", KT1[64:128], ksr[64:128], [], [BKT1])
        dma("sp", VA, vs.rearrange("(t p) c -> p t c", p=128), [], [BVA])
        gout, Bgout = sb([128, H * 64], F32)
        load_bc(gout, Bgout, (W["a_out_norm"] if kind == "a" else W["b_out_norm"])[l])
        if kind == "b":
            esk, Besk = sb([128, 4], F32)
            load_bc(esk, Besk, W["b_sink"][l])
            act(esk, esk, AF.Exp, [Besk], [Besk])
        QT = [sb([128, qw], BF16) for _ in range(2)]
        PT = [sb([128, 2 * qw], BF16) for _ in range(3)]
        oT, BoT = sb([65, 2, qw], F32)
        o_, Bo = sb([128, H, 64], F32)
        den, Bden = sb([128, H], F32)
        junk, Bj = sb([128, H * 64], BF16)
        ss, Bss = sb([128, 1], F32)
        rstd, Brs = sb([128, 1], F32)
        mixt = [sb([128, H * 64], BF16) for _ in range(2)]
        nsb = 2 if kind == "a" else 1
        steps = []
        for i in range(nt):
            klist = list(range(nt)) if kind == "a" else [c for c in (i - 1, i, i + 1) if 0 <= c < nt]
            for idx, c in enumerate(klist):
                steps.append((i, idx, c, idx == len(klist) - 1))

        def emit_S(n):
            i, idx, c, lastc = steps[n]
            Q_, BQ = QT[i % 2]
            if idx == 0:
                dma("sp", Q_, qs[i], [], [BQ])
            b0 = (n % 2) * nsb
            pss = ps(b0, nsb)
            banks = [PB[b0 + x] for x in range(nsb)]
            mm(pss[:, 0:qw], KT0[:, c, :], Q_, True, True, [BKT0, BQ], banks)
            mm(pss[:, qw:2 * qw], KT1[:, c, :], Q_, True, True, [BKT1, BQ], banks)

        def emit_rest(n):
            i, idx, c, lastc = steps[n]
            b0 = (n % 2) * nsb
            pss = ps(b0, nsb)
            banks = [PB[b0 + x] for x in range(nsb)]
            P_, BP = PT[n % 3]
            act(P_, pss[:, 0:2 * qw], AF.Exp, banks, [BP], scale=0.125)
            if kind == "b" and c != i:
                tri = triL if c < i else triR
                Btri = BtriL if c < i else BtriR
                P3 = P_.rearrange("p (h t) -> p h t", t=128)
                tt("dve", P3, P3, bc(tri.unsqueeze(1), [128, 2 * nj, 128]), ALU.mult, [BP, Btri], [BP])
            for g in range(2):
                mm(ps(4 + g)[0:65, 0:qw], VA[:, c, g * 65:(g + 1) * 65], P_[:, g * qw:(g + 1) * qw],
                   idx == 0, lastc, [BVA, BP], [PB[4 + g]])
            while pend and (lastc or n - pend[0][1] >= 2):
                epilogue2(pend.pop(0)[0])
            if lastc:
                epilogue(i)
                pend.append((i, n))

        pend = []

        def epilogue(i):
            for g in range(2):
                cp("dve", oT[:, g, :], ps(4 + g)[0:65, 0:qw], [PB[4 + g]], [BoT])
            for g in range(2):
                for j in range(nj):
                    tr(ps(6 + g)[:, j * 65:(j + 1) * 65], oT[0:65, g, j * 128:(j + 1) * 128], ident_f[0:65, 0:65],
                       [BoT, Bidf], [PB[6 + g]])
            for g in range(2):
                pe3 = ps(6 + g)[:, 0:nj * 65].rearrange("p (j c) -> p j c", c=65)
                dg = den[:, g * nj:(g + 1) * nj]
                if kind == "b":
                    tt("dve", dg, pe3[:, :, 64], esk[:, g * nj:(g + 1) * nj], ALU.add, [PB[6 + g], Besk], [Bden])
                    recip(dg, dg, [Bden], [Bden])
                else:
                    recip(dg, pe3[:, :, 64], [PB[6 + g]], [Bden])
                tt("dve", o_[:, g * nj:(g + 1) * nj, :], pe3[:, :, 0:64], bc(dg.unsqueeze(2), [128, nj, 64]), ALU.mult,
                   [PB[6 + g], Bden], [Bo])

        def epilogue2(i):
            of = o_.rearrange("p h d -> p (h d)")
            act(junk, of, AF.Square, [Bo], [Bj, Bss], accum_out=ss)
            act(ss, ss, AF.Sqrt, [Bss, Beps], [Bss], scale=1.0 / (H * 64), bias=epsb)
            recip(rstd, ss, [Bss], [Brs])
            m_, Bm = mixt[i % 2]
            stt("dve", m_, of, rstd[:, 0:1], gout, ALU.mult, ALU.mult, [Bo, Brs, Bgout], [Bm])
            dma("pool", q.mix[i * 128:(i + 1) * 128, col0:col0 + H * 64], m_, [Bm], [])

        emit_S(0)
        for n in range(len(steps)):
            if n + 1 < len(steps):
                emit_S(n + 1)
            emit_rest(n)
        while pend:
            epilogue2(pend.pop(0)[0])
        S.barrier()

    def retention(l, q):
        st["off"] = PERSIST
        nt = q.nt
        lgf, Blgf = sb([128, 4], F32)
        lgb, Blgb = sb([128, 4], F32)
        load_bc(lgf, Blgf, W["c_decay_fwd"][l])
        load_bc(lgb, Blgb, W["c_decay_bwd"][l])
        for t_, B_ in ((lgf, Blgf), (lgb, Blgb)):
            act(t_, t_, AF.Exp, [B_], [B_])
            ts("dve", t_, t_, -1.0, None, ALU.mult, None, [B_], [B_])
        lpf, Blpf = sb([128, 2], F32)
        lpb, Blpb = sb([128, 2], F32)
        for (dst, Bd, srcv, Bs) in ((lpf, Blpf, lgf, Blgf), (lpb, Blpb, lgb, Blgb)):
            s3 = srcv.rearrange("p (a b) -> p a b", b=2)
            cp("dve", dst[0:64, :], s3[0:64, :, 0], [Bs], [Bd])
            cp("dve", dst[64:128, :], s3[64:128, :, 1], [Bs, Bd], [Bd])
        t1, Bt1 = sb([128, 128], F32)
        t2, Bt2 = sb([128, 128], F32)
        maskT, Bmask = sb([128, 4, 128], BF16)
        for h in range(4):
            ts("dve", t1, posD, lgf[:, h:h + 1], None, ALU.mult, None, [BposD, Blgf], [Bt1])
            stt("dve", t2, negD, lgb[:, h:h + 1], t1, ALU.mult, ALU.add, [BnegD, Blgb, Bt1], [Bt2])
            act(t2, t2, AF.Exp, [Bt2], [Bt2])
            ts("dve", maskT[:, h, :], t2, 0.125, None, ALU.mult, None, [Bt2], [Bmask])
        qdf, Bqdf = sb([128, 2, 128], BF16)
        qdb, Bqdb = sb([128, 2, 128], BF16)
        for pr in range(2):
            act(t1, aidx1, AF.Exp, [Ba1, Blpf], [Bt1], scale=lpf[:, pr:pr + 1])
            ts("dve", qdf[:, pr, :], t1, 0.125, None, ALU.mult, None, [Bt1], [Bqdf])
            act(t2, aidxC, AF.Exp, [BaC, Blpb], [Bt2], scale=lpb[:, pr:pr + 1])
            ts("dve", qdb[:, pr, :], t2, 0.125, None, ALU.mult, None, [Bt2], [Bqdb])
        kdf, Bkdf = sb([128, 4], F32)
        kdb, Bkdb = sb([128, 4], F32)
        ts("dve", kdf, lgf, c127[:, 0:1], None, ALU.mult, None, [Blgf, Bc127], [Bkdf])
        act(kdf, kdf, AF.Exp, [Bkdf], [Bkdf])
        ts("dve", kdb, lgb, cidx[:, 0:1], None, ALU.mult, None, [Blgb, Bcidx], [Bkdb])
        act(kdb, kdb, AF.Exp, [Bkdb], [Bkdb])
        dCf, BdCf = sb([128, 2], F32)
        dCb, BdCb = sb([128, 2], F32)
        act(dCf, lpf, AF.Exp, [Blpf], [BdCf], scale=128.0)
        act(dCb, lpb, AF.Exp, [Blpb], [BdCb], scale=128.0)
        cgn, Bcgn = sb([128, 256], F32)
        load_bc(cgn, Bcgn, W["c_gn"][l])
        Sall, BSall = sb([128, nt, 128], BF16)
        kvbA, BkvbA = sb([128, nt, 128], F32)
        Sst, BS = sb([128, 128], F32)
        Rst, BR = sb([128, 128], F32)
        Rb, BRb = sb([128, 128], BF16)
        memset("pool", Sst, 0.0, [BS])
        memset("pool", Rst, 0.0, [BR])
        kt = [sb([128, 256], BF16) for _ in range(2)]
        vt = [sb([128, 512], BF16) for _ in range(2)]
        kwf, Bkwf = sb([128, 256], BF16)
        kwb, Bkwb = sb([128, 256], BF16)
        for n in range(nt):
            k_, Bk = kt[n % 2]
            v_, Bv = vt[n % 2]
            rows = slice(n * 128, (n + 1) * 128)
            dma("sp", k_, q.kct[rows, :], [], [Bk])
            dma("sp", v_, q.vcg[rows, :], [], [Bv])
            k3 = k_.rearrange("p (h d) -> p h d", d=64)
            tt("dve", kwf.rearrange("p (h d) -> p h d", d=64), k3, bc(kdf.unsqueeze(2), [128, 4, 64]), ALU.mult,
               [Bk, Bkdf], [Bkwf])
            tt("pool", kwb.rearrange("p (h d) -> p h d", d=64), k3, bc(kdb.unsqueeze(2), [128, 4, 64]), ALU.mult,
               [Bk, Bkdb], [Bkwb])
            for pr in range(2):
                mm(ps(0)[:, pr * 256:(pr + 1) * 256], kwf[:, pr * 128:(pr + 1) * 128], v_[:, 0:256], True, True,
                   [Bkwf, Bv], [PB[0]])
                mm(ps(1)[:, pr * 256:(pr + 1) * 256], kwb[:, pr * 128:(pr + 1) * 128], v_[:, 0:256], True, True,
                   [Bkwb, Bv], [PB[1]])
            cp("act", Sall[:, n, :], Sst, [BS], [BSall])
            S3 = Sst.rearrange("p (a e) -> p a e", e=64)
            tt("dve", S3, S3, bc(dCf.unsqueeze(2), [128, 2, 64]), ALU.mult, [BS, BdCf, BSall], [BS])
            for h2 in range(2):
                r_ = slice(h2 * 64, (h2 + 1) * 64)
                blkf = psum[r_, h2 * 64:h2 * 64 + 768].rearrange("p (a x) -> p a x", x=384)[:, :, 0:64]
                blkb = psum[r_, 512 + h2 * 64:512 + h2 * 64 + 768].rearrange("p (a x) -> p a x", x=384)[:, :, 0:64]
                tt("dve", S3[r_], S3[r_], blkf, ALU.add, [BS, PB[0]], [BS])
                cp("act", kvbA[r_, n, :].rearrange("p (a e) -> p a e", e=64), blkb, [PB[1]], [BkvbA])
        qz = [[sb([128, 256], BF16) for _ in range(2)] for _ in range(2)]
        kT = [sb([128, 256], BF16) for _ in range(2)]
        vg = [sb([128, 512], BF16) for _ in range(2)]
        Qfz = [sb([128, 256], BF16) for _ in range(2)]
        Qbz = [sb([128, 256], BF16) for _ in range(2)]
        for h2 in range(2):
            oth = slice((1 - h2) * 64, (2 - h2) * 64)
            for (t_, B_) in qz[h2] + [Qfz[h2], Qbz[h2]]:
                memset("pool", t_[oth], 0.0, [B_])
        PTm, BPTm = sb([128, 512], BF16)
        oT, BoT = sb([64, 512], F32)
        oc, Boc = sb([128, 4, 64], F32)
        sqj, Bsqj = sb([128, 4, 64], F32)
        sm, Bsm = sb([128, 4], F32)
        vs_, Bvs = sb([128, 4], F32)
        mixt = [sb([128, 256], BF16) for _ in range(2)]
        for ii, n in enumerate(range(nt - 1, -1, -1)):
            k_, Bk = kT[ii % 2]
            v_, Bv = vg[ii % 2]
            rows = slice(n * 128, (n + 1) * 128)
            dma("sp", k_, q.kc[n], [], [Bk])
            dma("sp", v_, q.vcg[rows, :], [], [Bv])
            cp("act", Rb, Rst, [BR], [BRb])
            for h2 in range(2):
                r_ = slice(h2 * 64, (h2 + 1) * 64)
                qh, Bqh = qz[h2][ii % 2]
                dma("sp", qh[r_], q.qc[n][r_], [], [Bqh])
                q3 = qh[r_].rearrange("p (a t) -> p a t", t=128)
                tt("dve", Qfz[h2][0][r_].rearrange("p (a t) -> p a t", t=128), q3, qdf[r_], ALU.mult,
                   [Bqh, Bqdf], [Qfz[h2][1]])
                tt("pool", Qbz[h2][0][r_].rearrange("p (a t) -> p a t", t=128), q3, qdb[r_], ALU.mult,
                   [Bqh, Bqdb], [Qbz[h2][1]])
            for h in range(4):
                pr, h2 = h // 2, h % 2
                qh, Bqh = qz[h2][ii % 2]
                mm(ps(2)[:, h * 128:(h + 1) * 128], k_[:, pr * 128:(pr + 1) * 128], qh[:, pr * 128:(pr + 1) * 128],
                   True, True, [Bk, Bqh], [PB[2]])
            tt("dve", PTm.rearrange("p (h t) -> p h t", t=128), ps(2).rearrange("p (h t) -> p h t", t=128), maskT,
               ALU.mult, [PB[2], Bmask], [BPTm])
            for h in range(4):
                pr, h2 = h // 2, h % 2
                r_ = slice(h2 * 64, (h2 + 1) * 64)
                po = ps(3)[0:64, h * 128:(h + 1) * 128]
                mm(po, v_[:, h * 64:(h + 1) * 64], PTm[:, h * 128:(h + 1) * 128], True, False, [Bv, BPTm], [PB[3]])
                mm(po, Sall[:, n, pr * 64:(pr + 1) * 64], Qfz[h2][0][:, pr * 128:(pr + 1) * 128], False, False,
                   [BSall, Qfz[h2][1]], [PB[3]])
                mm(po, Rb[:, pr * 64:(pr + 1) * 64], Qbz[h2][0][:, pr * 128:(pr + 1) * 128], False, True,
                   [BRb, Qbz[h2][1]], [PB[3]])
            R3 = Rst.rearrange("p (a e) -> p a e", e=64)
            tt("dve", R3, R3, bc(dCb.unsqueeze(2), [128, 2, 64]), ALU.mult, [BR, BdCb, BRb], [BR])
            tt("dve", Rst, Rst, kvbA[:, n, :], ALU.add, [BR, BkvbA], [BR])
            cp("act", oT, ps(3)[0:64, :], [PB[3]], [BoT])
            for h in range(4):
                tr(ps(4)[:, h * 64:(h + 1) * 64], oT[0:64, h * 128:(h + 1) * 128], ident_f[0:64, 0:64],
                   [BoT, Bidf], [PB[4]])
            pe3 = ps(4)[:, 0:256].rearrange("p (h e) -> p h e", e=64)
            rsum(sm, pe3, [PB[4]], [Bsm])
            ts("dve", sm, sm, 1.0 / 64, None, ALU.mult, None, [Bsm], [Bsm])
            tt("dve", oc, pe3, bc(sm.unsqueeze(2), [128, 4, 64]), ALU.subtract, [PB[4], Bsm], [Boc])
            act(sqj, oc, AF.Square, [Boc], [Bsqj])
            rsum(vs_, sqj, [Bsqj], [Bvs])
            act(vs_, vs_, AF.Sqrt, [Bvs, Beps], [Bvs], scale=1.0 / 64, bias=epsb)
            recip(vs_, vs_, [Bvs], [Bvs])
            tt("dve", oc, oc, bc(vs_.unsqueeze(2), [128, 4, 64]), ALU.mult, [Boc, Bvs], [Boc])
            ocf = oc.rearrange("p h e -> p (h e)")
            tt("dve", ocf, ocf, cgn, ALU.mult, [Boc, Bcgn], [Boc])
            m_, Bm = mixt[ii % 2]
            tt("dve", m_, ocf, v_[:, 256:512], ALU.mult, [Boc, Bv], [Bm])
            dma("pool", q.mix[rows, 768:1024], m_, [Bm], [])
        S.barrier()

    def phaseCD(l):
        st["off"] = PERSIST
        wout, Bwout = sb([128, KC, D], BF16)
        wxq, Bwxq = sb([128, KC, D], BF16)
        wxo, Bwxo = sb([128, KC, D], BF16)
        load_w(wout, Bwout, W["w_out"][l], KC)
        load_w(wxq, Bwxq, W["w_xq"][l], KC)
        load_w(wxo, Bwxo, W["w_xo"][l], KC)
        gcr, Bgcr = sb([128, D], F32)
        load_bc(gcr, Bgcr, W["g_cross"][l])
        mKT = [sb([128, KC, MEM], BF16) for _ in seqs]
        mV = [sb([128, 2, D], BF16) for _ in seqs]
        keep = st["off"]
        wxk, Bwxk = sb([128, KC, D], BF16)
        wxv, Bwxv = sb([128, KC, D], BF16)
        load_w(wxk, Bwxk, W["w_xk"][l], KC)
        load_w(wxv, Bwxv, W["w_xv"][l], KC)
        gme, Bgme = sb([128, D], F32)
        load_bc(gme, Bgme, W["g_mem"][l])
        mt_, Bmt = sb([128, D], F32)
        junk = sb([128, D], BF16)
        ss = sb([128, 1], F32)
        rstd = sb([128, 1], F32)
        scr = (junk[0], junk[1], ss[0], ss[1], rstd[0], rstd[1])
        hb, Bhb = sb([128, D], BF16)
        mT, BmT = sb([128, KC, MEM], BF16)
        for q in seqs:
            for c in range(2):
                dma("sp", mt_, q.mem[c * 128:(c + 1) * 128, :], [], [Bmt])
                rmsnorm_to_bf16(mt_, Bmt, gme, Bgme, hb, Bhb, scr)
                transpose8(hb, Bhb, mT[:, :, c * 128:(c + 1) * 128], BmT, 0, "act")
            K_, BK = mKT[q.i]
            V_, BV = mV[q.i]
            for f in range(KC):
                b = 1 + f % 2
                for k in range(KC):
                    mm(ps(b)[:, 0:MEM], wxk[:, k, f * 128:(f + 1) * 128], mT[:, k, :], k == 0, k == KC - 1,
                       [Bwxk, BmT], [PB[b]])
                cp("act", K_[:, f, :], ps(b)[:, 0:MEM], [PB[b]], [BK])
            for c in range(2):
                for hf in range(2):
                    b = 3 + hf
                    for k in range(KC):
                        mm(ps(b), mT[:, k, c * 128:(c + 1) * 128], wxv[:, k, hf * 512:(hf + 1) * 512], k == 0,
                           k == KC - 1, [BmT, Bwxv], [PB[b]])
                    cp("dve", V_[:, c, hf * 512:(hf + 1) * 512], ps(b), [PB[b]], [BV])
        S.barrier()
        st["off"] = keep
        G = 4
        xt = [sb([128, D], F32) for _ in range(G)]
        mx = [sb([128, D], BF16) for _ in range(2)]
        junk = sb([128, D], BF16)
        ss = sb([128, 1], F32)
        rstd = sb([128, 1], F32)
        scr = (junk[0], junk[1], ss[0], ss[1], rstd[0], rstd[1])
        hb, Bhb = sb([128, D], BF16)
        mixT, BmixT = sb([128, KC, 128], BF16)
        hT, BhT = sb([128, KC, 512], BF16)
        QT, BQT = sb([128, KC, 512], BF16)
        PT, BPT = sb([128, 8, 512], BF16)
        OT, BOT = sb([128, KC, 512], BF16)
        rden, Brden = sb([128, 512], F32)
        for q in seqs:
            src = q.x_in if l == 0 else q.xs
            K_, BK = mKT[q.i]
            V_, BV = mV[q.i]
            for gi in range(q.nt // G):
                for j in range(G):
                    t = gi * G + j
                    rows = slice(t * 128, (t + 1) * 128)
                    x_, Bx = xt[j]
                    m_, Bm = mx[j % 2]
                    dma("sp", x_, src[rows, :], [], [Bx])
                    dma("sp", m_, q.mix[rows, :], [], [Bm])
                    transpose8(m_, Bm, mixT, BmixT, 0, "act")
                    for hf in range(2):
                        for k in range(KC):
                            mm(ps(1 + hf), mixT[:, k, :], wout[:, k, hf * 512:(hf + 1) * 512], k == 0, k == KC - 1,
                               [BmixT, Bwout], [PB[1 + hf]])
                    tt("dve", x_, x_, ps(1, 2), ALU.add, [Bx, PB[1], PB[2]], [Bx])
                    rmsnorm_to_bf16(x_, Bx, gcr, Bgcr, hb, Bhb, scr)
                    transpose8(hb, Bhb, hT[:, :, j * 128:(j + 1) * 128], BhT, 0, "act")
                for f in range(KC):
                    b = 3 + f % 2
                    for k in range(KC):
                        mm(ps(b), wxq[:, k, f * 128:(f + 1) * 128], hT[:, k, :], k == 0, k == KC - 1,
                           [Bwxq, BhT], [PB[b]])
                    cp("act" if f % 2 else "dve", QT[:, f, :], ps(b), [PB[b]], [BQT])
                for h in range(4):
                    for mc in range(2):
                        b = 5 + mc
                        for dc in range(2):
                            mm(ps(b), K_[:, 2 * h + dc, mc * 128:(mc + 1) * 128], QT[:, 2 * h + dc, :], dc == 0,
                               dc == 1, [BK, BQT], [PB[b]])
                        act(PT[:, 2 * h + mc, :], ps(b), AF.Exp, [PB[b]], [BPT], scale=1.0 / 16)
                for h in range(4):
                    for mc in range(2):
                        mm(ps(7), ones_b, PT[:, 2 * h + mc, :], mc == 0, mc == 1, [Bones, BPT], [PB[7]])
                    recip(rden, ps(7), [PB[7]], [Brden])
                    for dc in range(2):
                        b = 3 + dc
                        f = 2 * h + dc
                        for mc in range(2):
                            mm(ps(b), V_[:, mc, f * 128:(f + 1) * 128], PT[:, 2 * h + mc, :], mc == 0, mc == 1,
                               [BV, BPT], [PB[b]])
                        tt("dve", OT[:, f, :], ps(b), rden, ALU.mult, [PB[b], Brden], [BOT])
                for j in range(G):
                    t = gi * G + j
                    x_, Bx = xt[j]
                    for hf in range(2):
                        for f in range(KC):
                            mm(ps(1 + hf), OT[:, f, j * 128:(j + 1) * 128], wxo[:, f, hf * 512:(hf + 1) * 512], f == 0,
                               f == KC - 1, [BOT, Bwxo], [PB[1 + hf]])
                    tt("dve", x_, x_, ps(1, 2), ALU.add, [Bx, PB[1], PB[2]], [Bx])
                    dma("pool", q.xs[t * 128:(t + 1) * 128, :], x_, [Bx], [])
        S.barrier()

    def phaseE(l):
        st["off"] = PERSIST
        last = l == DEPTH - 1
        wg, Bwg = sb([128, KC, DFF], BF16)
        wu, Bwu = sb([128, KC, DFF], BF16)
        wd, Bwd = sb([128, FC, D], BF16)
        load_w(wg, Bwg, W["w_gate"][l], KC)
        load_w(wu, Bwu, W["w_up"][l], KC)
        load_w(wd, Bwd, W["w_down"][l], FC)
        gff, Bgff = sb([128, D], F32)
        load_bc(gff, Bgff, W["g_ffn"][l])
        if last:
            gfi, Bgfi = sb([128, D], F32)
            load_bc(gfi, Bgfi, W["g_final"])
        G = 4
        xt = [sb([128, D], F32) for _ in range(2)]
        junk = sb([128, D], BF16)
        ss = sb([128, 1], F32)
        rstd = sb([128, 1], F32)
        scr = (junk[0], junk[1], ss[0], ss[1], rstd[0], rstd[1])
        hb, Bhb = sb([128, D], BF16)
        hT, BhT = sb([128, KC, 512], BF16)
        sg = [sb([128, 512], F32) for _ in range(2)]
        aT, BaT = sb([128, FC, 512], BF16)
        for q in seqs:
            for gi in range(q.nt // G):
                for j in range(G):
                    t = gi * G + j
                    x_, Bx = xt[j % 2]
                    dma("sp", x_, q.xs[t * 128:(t + 1) * 128, :], [], [Bx])
                    rmsnorm_to_bf16(x_, Bx, gff, Bgff, hb, Bhb, scr)
                    transpose8(hb, Bhb, hT[:, :, j * 128:(j + 1) * 128], BhT, 0, "act")
                for f in range(FC):
                    bg, bu = 1 + f % 2, 3 + f % 2
                    for k in range(KC):
                        mm(ps(bg), wg[:, k, f * 128:(f + 1) * 128], hT[:, k, :], k == 0, k == KC - 1,
                           [Bwg, BhT], [PB[bg]])
                    for k in range(KC):
                        mm(ps(bu), wu[:, k, f * 128:(f + 1) * 128], hT[:, k, :], k == 0, k == KC - 1,
                           [Bwu, BhT], [PB[bu]])
                    s_, Bs = sg[f % 2]
                    act(s_, ps(bg), AF.Silu, [PB[bg]], [Bs])
                    tt("dve", aT[:, f, :], s_, ps(bu), ALU.mult, [Bs, PB[bu]], [BaT])
                for j in range(G):
                    t = gi * G + j
                    x_, Bx = xt[j % 2]
                    dma("sp", x_, q.xs[t * 128:(t + 1) * 128, :], [], [Bx])
                    for hf in range(2):
                        for f in range(FC):
                            mm(ps(5 + hf), aT[:, f, j * 128:(j + 1) * 128], wd[:, f, hf * 512:(hf + 1) * 512], f == 0,
                               f == FC - 1, [BaT, Bwd], [PB[5 + hf]])
                    tt("dve", x_, x_, ps(5, 2), ALU.add, [Bx, PB[5], PB[6]], [Bx])
                    if last:
                        act(scr[0], x_, AF.Square, [Bx], [scr[1], scr[3]], accum_out=scr[2])
                        act(scr[2], scr[2], AF.Sqrt, [scr[3], Beps], [scr[3]], scale=1.0 / D, bias=epsb)
                        recip(scr[4], scr[2], [scr[3]], [scr[5]])
                        stt("dve", x_, x_, scr[4][:, 0:1], gfi, ALU.mult, ALU.mult, [Bx, scr[5], Bgfi], [Bx])
                        dma("pool", q.y[t * 128:(t + 1) * 128, :], x_, [Bx], [])
                    else:
                        dma("pool", q.xs[t * 128:(t + 1) * 128, :], x_, [Bx], [])
        S.barrier()

    setup()
    for l in range(DEPTH):
        phaseA(l)
        for q in seqs:
            attn(l, q, "a")
            attn(l, q, "b")
            retention(l, q)
        phaseCD(l)
        phaseE(l)
    S.emit()
    es.close()
    return nc


WNAMES = ("g_mix", "w_in", "a_q_norm", "a_k_norm", "a_out_norm", "b_sink", "b_out_norm", "c_decay_fwd",
          "c_decay_bwd", "c_gn", "w_out", "g_cross", "g_mem", "w_xq", "w_xk", "w_xv", "w_xo", "g_ffn", "w_gate",
          "w_up", "w_down", "g_final")


def make_in_maps(inputs, n_cores=8):
    xp, xsm = inputs["x_prompt"], inputs["x_sample"]
    mp, ms = inputs["mem_prompt"], inputs["mem_sample"]
    wts = {k: np.ascontiguousarray(np.asarray(inputs[k], dtype=np.float32)) for k in WNAMES}
    maps = []
    for c in range(n_cores):
        m = dict(wts)
        m["x0"] = np.ascontiguousarray(xp[c % xp.shape[0]], dtype=np.float32)
        m["m0"] = np.ascontiguousarray(mp[c % mp.shape[0]], dtype=np.float32)
        m["x1"] = np.ascontiguousarray(xsm[c % xsm.shape[0]], dtype=np.float32)
        m["m1"] = np.ascontiguousarray(ms[c % ms.shape[0]], dtype=np.float32)
        maps.append(m)
    return maps


def kernel(**inputs):
    xp, xsm = np.asarray(inputs["x_prompt"]), np.asarray(inputs["x_sample"])
    depth = np.asarray(inputs["w_in"]).shape[0]
    nc = build(xp.shape[1], xsm.shape[1], depth)
    maps = make_in_maps({k: np.asarray(v) for k, v in inputs.items()})
    res = run_bass_kernel_spmd(nc, maps, core_ids=list(range(8)))
    yp = np.stack([res.results[b]["y0"] for b in range(xp.shape[0])], axis=0).astype(np.float32)
    ysm = np.stack([res.results[b]["y1"] for b in range(xsm.shape[0])], axis=0).astype(np.float32)
    return (yp, ysm)
```
